# Optimizing a Trainium2 kernel written in Bass

```python
import jax, jax.numpy as jnp
from jax import lax
import numpy as np

D_MODEL = 1024
BATCH = 16
SEQ = 4096
DEPTH = 4

HEAD_DIM = 64
N_Q_A = 16
N_KV_A = 2
GROUP_A = N_Q_A // N_KV_A
WINDOW = 128
BLOCK = 128
N_H_B = 16
ROT_DIM = HEAD_DIM // 4
ROPE_THETA = 500000.0
D_FF = -(-8 * D_MODEL // (3 * 256)) * 256
N_MIXERS = 2
N_A = (DEPTH + 1) // 2
N_B = DEPTH // 2
QKV_A = (N_Q_A + 2 * N_KV_A) * HEAD_DIM
QKV_B = 3 * N_H_B * HEAD_DIM
EPS = 1e-6

kernel_name = 'hybrid_swa_sink_stickbreak_block'


def rmsnorm(x, gain):
    xf = x.astype(jnp.float32)
    y = xf * lax.rsqrt(jnp.mean(xf * xf, axis=-1, keepdims=True) + EPS)
    return (y * gain.astype(jnp.float32)).astype(x.dtype)


def partial_rope(x, positions):
    half = ROT_DIM // 2
    inv_freq = jnp.power(jnp.float32(ROPE_THETA), -jnp.arange(half, dtype=jnp.float32) * 2.0 / ROT_DIM)
    ang = positions.astype(jnp.float32)[:, :, None, None] * inv_freq
    cos, sin = jnp.cos(ang), jnp.sin(ang)
    xr = x[..., :ROT_DIM].astype(jnp.float32)
    x1, x2 = xr[..., :half], xr[..., half:]
    rot = jnp.concatenate([x1 * cos - x2 * sin, x2 * cos + x1 * sin], axis=-1).astype(x.dtype)
    return jnp.concatenate([rot, x[..., ROT_DIM:]], axis=-1)


def sliding_window_sink_attention(h, positions, w_qkv, q_gain, k_gain, sinks, w_o):
    B, S, _ = h.shape
    qkv = h @ w_qkv
    q, k, v = jnp.split(qkv, [N_Q_A * HEAD_DIM, (N_Q_A + N_KV_A) * HEAD_DIM], axis=-1)
    q = q.reshape(B, S, N_Q_A, HEAD_DIM)
    k = k.reshape(B, S, N_KV_A, HEAD_DIM)
    v = v.reshape(B, S, N_KV_A, HEAD_DIM)
    q = partial_rope(rmsnorm(q, q_gain), positions)
    k = partial_rope(rmsnorm(k, k_gain), positions)
    q = q.reshape(B, S, N_KV_A, GROUP_A, HEAD_DIM)
    pad = jnp.zeros((B, BLOCK, N_KV_A, HEAD_DIM), k.dtype)
    kp = jnp.concatenate([pad, k], axis=1)
    vp = jnp.concatenate([pad, v], axis=1)
    scale = HEAD_DIM ** -0.5
    q_idx = jnp.arange(BLOCK)[:, None] + BLOCK
    k_idx = jnp.arange(2 * BLOCK)[None, :]
    rel = q_idx - k_idx
    band = (rel >= 0) & (rel < WINDOW)
    sink_logit = sinks.astype(jnp.float32).reshape(1, N_KV_A, GROUP_A, 1, 1)

    def block_fn(i):
        start = i * BLOCK
        qb = lax.dynamic_slice_in_dim(q, start, BLOCK, axis=1)
        kb = lax.dynamic_slice_in_dim(kp, start, 2 * BLOCK, axis=1)
        vb = lax.dynamic_slice_in_dim(vp, start, 2 * BLOCK, axis=1)
        s = jnp.einsum('bqkgd,bskd->bkgqs', qb, kb).astype(jnp.float32) * scale
        valid = band & (start - BLOCK + k_idx >= 0)
        s = jnp.where(valid, s, -jnp.inf)
        sink_col = jnp.broadcast_to(sink_logit, s.shape[:-1] + (1,))
        p = jax.nn.softmax(jnp.concatenate([s, sink_col], axis=-1), axis=-1)[..., :-1]
        o = jnp.einsum('bkgqs,bskd->bqkgd', p.astype(vb.dtype), vb)
        return o.reshape(B, BLOCK, N_Q_A * HEAD_DIM)

    out = lax.map(block_fn, jnp.arange(S // BLOCK))
    out = jnp.moveaxis(out, 0, 1).reshape(B, S, N_Q_A * HEAD_DIM)
    return out @ w_o


def stick_breaking_attention(h, w_qkv, w_o):
    B, S, _ = h.shape
    qkv = h @ w_qkv
    q, k, v = jnp.split(qkv, 3, axis=-1)
    q = q.reshape(B, S, N_H_B, HEAD_DIM)
    k = k.reshape(B, S, N_H_B, HEAD_DIM)
    v = v.reshape(B, S, N_H_B, HEAD_DIM)
    scale = HEAD_DIM ** -0.5
    k_idx = jnp.arange(S)[None, :]

    def block_fn(i):
        start = i * BLOCK
        qb = lax.dynamic_slice_in_dim(q, start, BLOCK, axis=1)
        z = jnp.einsum('bqhd,bshd->bhqs', qb, k).astype(jnp.float32) * scale
        t_idx = start + jnp.arange(BLOCK)[:, None]
        strict = k_idx < t_idx
        log_beta = jax.nn.log_sigmoid(z)
        log_one_minus = jnp.where(strict, jax.nn.log_sigmoid(-z), 0.0)
        rc = lax.cumsum(log_one_minus, axis=3, reverse=True)
        after = jnp.pad(rc[..., 1:], ((0, 0), (0, 0), (0, 0), (0, 1)))
        a = jnp.where(strict, jnp.exp(log_beta + after), 0.0)
        o = jnp.einsum('bhqs,bshd->bqhd', a.astype(v.dtype), v)
        return o.reshape(B, BLOCK, N_H_B * HEAD_DIM)

    out = lax.map(block_fn, jnp.arange(S // BLOCK))
    out = jnp.moveaxis(out, 0, 1).reshape(B, S, N_H_B * HEAD_DIM)
    return out @ w_o


def swiglu(h, w_gate, w_up, w_down):
    return (jax.nn.silu(h @ w_gate) * (h @ w_up)) @ w_down


def setup_inputs(seed: int = 0) -> dict:
    key = jax.random.key(seed)
    ks = jax.random.split(key, 20)
    f32 = jnp.float32
    nrm = lambda k, shape, s: jax.random.normal(k, shape, f32) * s
    x = jax.random.normal(ks[0], (BATCH, SEQ, D_MODEL), f32)
    c = jax.random.normal(ks[1], (BATCH, D_MODEL), f32)
    offset = jax.random.randint(ks[2], (BATCH, 1), 0, 4096, dtype=jnp.int32)
    positions = offset + jnp.arange(SEQ, dtype=jnp.int32)[None, :]
    return {
        'x': x,
        'c': c,
        'positions': positions,
        'ada_w': nrm(ks[3], (DEPTH, D_MODEL, 6 * D_MODEL), 0.5 * D_MODEL ** -0.5),
        'ada_b': nrm(ks[4], (DEPTH, 6 * D_MODEL), 0.01),
        'norm1_g': 1.0 + nrm(ks[5], (DEPTH, D_MODEL), 0.05),
        'norm2_g': 1.0 + nrm(ks[6], (DEPTH, D_MODEL), 0.05),
        'wqkv_a': nrm(ks[7], (N_A, D_MODEL, QKV_A), D_MODEL ** -0.5),
        'q_norm_a': 1.0 + nrm(ks[8], (N_A, HEAD_DIM), 0.05),
        'k_norm_a': 1.0 + nrm(ks[9], (N_A, HEAD_DIM), 0.05),
        'sinks_a': nrm(ks[10], (N_A, N_Q_A), 1.0),
        'wo_a': nrm(ks[11], (N_A, N_Q_A * HEAD_DIM, D_MODEL), (N_Q_A * HEAD_DIM) ** -0.5),
        'wqkv_b': nrm(ks[12], (N_B, D_MODEL, QKV_B), D_MODEL ** -0.5),
        'wo_b': nrm(ks[13], (N_B, N_H_B * HEAD_DIM, D_MODEL), (N_H_B * HEAD_DIM) ** -0.5),
        'w_gate': nrm(ks[14], (DEPTH, D_MODEL, D_FF), D_MODEL ** -0.5),
        'w_up': nrm(ks[15], (DEPTH, D_MODEL, D_FF), D_MODEL ** -0.5),
        'w_down': nrm(ks[16], (DEPTH, D_FF, D_MODEL), D_FF ** -0.5),
    }


def reference(x, c, positions, ada_w, ada_b, norm1_g, norm2_g, wqkv_a, q_norm_a, k_norm_a,
              sinks_a, wo_a, wqkv_b, wo_b, w_gate, w_up, w_down):
    cond = jax.nn.silu(c)
    for i in range(DEPTH):
        mod = (cond @ ada_w[i] + ada_b[i])[:, None, :]
        sh1, sc1, g1, sh2, sc2, g2 = jnp.split(mod, 6, axis=-1)
        h = rmsnorm(x, norm1_g[i]) * (1.0 + sc1) + sh1
        j = i // N_MIXERS
        if i % N_MIXERS == 0:
            y = sliding_window_sink_attention(h, positions, wqkv_a[j], q_norm_a[j], k_norm_a[j],
                                              sinks_a[j], wo_a[j])
        else:
            y = stick_breaking_attention(h, wqkv_b[j], wo_b[j])
        x = x + g1 * y
        h = rmsnorm(x, norm2_g[i]) * (1.0 + sc2) + sh2
        x = x + g2 * swiglu(h, w_gate[i], w_up[i], w_down[i])
    return x
```

```python
import numpy as np
import ml_dtypes
from contextlib import ExitStack
import concourse.bass as bass
import concourse.mybir as mybir
from concourse.bass_utils import run_bass_kernel_spmd

F32 = mybir.dt.float32
BF16 = mybir.dt.bfloat16
I32 = mybir.dt.int32
AF = mybir.ActivationFunctionType
ALU = mybir.AluOpType
AX = mybir.AxisListType

D = 1024
DFF = 2816
NFC = DFF // 128
HD = 64
EPS = 1e-6
QKVA = 1280
QKVB = 3072
N_CORES = 8
SEQ = 4096
BATCH = 16
ROPE_THETA = 500000.0

SB_BASE = 16512
SB_LIMIT = 229344
DEBUG_ARENA = False


class Buf:
    __slots__ = ("name", "w", "r", "wf", "dsem", "dcnt")

    def __init__(self, name):
        self.name = name
        self.w = {}
        self.r = {}
        self.wf = {}
        self.dsem = None
        self.dcnt = 0


class T:
    __slots__ = ("h", "b")

    def __init__(self, h, name):
        self.h = h
        self.b = Buf(name)

    def __getitem__(self, k):
        return self.h[k]


class Sched:
    ENGS = ("pe", "act", "dve", "pool", "sp")

    def __init__(self, nc, stack):
        self.nc = nc
        self.stack = stack
        self.sems = {}
        self.owner = {}
        self.dmax = {}
        self.free = []
        self.cnt = {}
        self.prog = {e: [] for e in self.ENGS}
        self.seen = {e: {} for e in self.ENGS}
        for e in self.ENGS:
            k = "E_" + e
            self.sems[k] = stack.enter_context(nc.semaphore("sem_" + e))
            self.cnt[e] = 0
        self.nd = 0
        self.ninst = 0

    def _dsem(self, b):
        if b.dsem is None:
            if self.free:
                k = self.free.pop()
                b.dcnt = self.dmax[k]
            else:
                k = "D%d" % self.nd
                self.nd += 1
                self.sems[k] = self.stack.enter_context(self.nc.semaphore("d%d" % self.nd))
                self.dmax[k] = 0
            self.owner[k] = b
            b.dsem = k
        return b.dsem

    def release(self, b):
        if b.dsem is not None:
            self.free.append(b.dsem)
            b.dsem = None

    def _need(self, eng, reads, writes, partial):
        need = {}
        for b in reads:
            for k, v in b.w.items():
                if need.get(k, 0) < v:
                    need[k] = v
        for b in writes:
            for k, v in b.r.items():
                if need.get(k, 0) < v:
                    need[k] = v
            for k, v in (b.wf if partial else b.w).items():
                if need.get(k, 0) < v:
                    need[k] = v
        out = []
        seen = self.seen[eng]
        for k, v in need.items():
            if k in self.dmax:
                v = max(v, self.dmax[k])
            if eng == "pe" and k == "E_pe":
                continue
            if seen.get(k, 0) >= v:
                continue
            seen[k] = v
            out.append((k, v))
        return out

    def _commit(self, key, val, reads, writes, partial):
        for b in reads:
            if b.r.get(key, 0) < val:
                b.r[key] = val
        for b in writes:
            if partial:
                if b.w.get(key, 0) < val:
                    b.w[key] = val
            else:
                b.w = {key: val}
                b.wf = {key: val}
                b.r = {}

    def op(self, eng, fn, reads=(), writes=(), partial=False):
        waits = self._need(eng, reads, writes, partial)
        self.cnt[eng] += 1
        key = "E_" + eng
        self.prog[eng].append((waits, fn, key, 1))
        self._commit(key, self.cnt[eng], reads, writes, partial)
        self.ninst += 1 + len(waits)

    def dma(self, q, out, in_, sb, reads=(), writes=(), partial=False, chain=False, **kw):
        key = self._dsem(sb)
        waits = self._need(q, reads, writes, partial)
        if not chain and self.seen[q].get(key, 0) < sb.dcnt:
            self.seen[q][key] = sb.dcnt
            waits = [w for w in waits if w[0] != key] + [(key, sb.dcnt)]
        sb.dcnt += 16
        self.dmax[key] = sb.dcnt
        self.prog[q].append((waits, (lambda e: e.dma_start(out=out, in_=in_, **kw)), key, 16))
        self._commit(key, sb.dcnt, reads, writes, partial)
        self.ninst += 1 + len(waits)

    def barrier(self):
        for e in self.ENGS:
            waits = []
            seen = self.seen[e]
            for k in self.sems:
                if k.startswith("E_"):
                    tgt = self.cnt[k[2:]]
                else:
                    tgt = self.dmax[k]
                if k == "E_" + e and e == "pe":
                    pass
                if seen.get(k, 0) < tgt:
                    seen[k] = tgt
                    waits.append((k, tgt))
            if waits:
                self.prog[e].append((waits, None, None, 0))
                self.ninst += len(waits)

    def emit(self):
        nc = self.nc
        engmap = {"pe": "tensor", "act": "scalar", "dve": "vector", "pool": "gpsimd", "sp": "sync"}
        with nc.Block() as block:
            for e in self.ENGS:
                prog = self.prog[e]

                def body(eng, prog=prog):
                    for waits, fn, key, inc in prog:
                        for k, v in waits:
                            eng.wait_ge(self.sems[k], v)
                        if fn is not None:
                            fn(eng).then_inc(self.sems[key], inc)
                getattr(block, engmap[e])(body)


class Arena:
    def __init__(self, nc, S, base, limit):
        self.nc = nc
        self.S = S
        self.top = base
        self.limit = limit
        self.tiles = []
        self.n = 0

    def alloc(self, name, shape, dt):
        isz = 4 if dt in (F32, I32) else 2
        n = 1
        for s in shape[1:]:
            n *= s
        size = (n * isz + 31) // 32 * 32
        assert self.top + size <= self.limit, ("SBUF overflow", name, self.top, size, self.limit)
        self.n += 1
        h = self.nc.alloc_sbuf_tensor_at("%s_%d" % (name, self.n), list(shape), dt, offset=self.top)
        self.top += size
        t = T(h, name)
        self.tiles.append(t)
        return t

    def mark(self):
        return (self.top, len(self.tiles))

    def reset(self, mk):
        self.peak = max(getattr(self, "peak", 0), self.top)
        if DEBUG_ARENA:
            print("arena reset: top", self.top, "limit", self.limit, "free", self.limit - self.top)
        top, n = mk
        for t in self.tiles[n:]:
            self.S.release(t.b)
        del self.tiles[n:]
        self.top = top


def build(S_LEN=SEQ, NSEQ=2, layers=(0, 1, 2, 3), do_prologue=True):
    NT = S_LEN // 128
    NCH = S_LEN // 512
    nc = bass.Bass("TRN2", target_bir_lowering=False)

    def din(name, shape, dt=F32):
        return nc.dram_tensor(name, list(shape), dt, kind="ExternalInput").ap()

    x_d = din("x", [NSEQ * S_LEN, D])
    c_d = din("c", [NSEQ, D])
    pos_d = din("positions", [NSEQ, S_LEN], I32)
    ada_w_d = din("ada_w", [4, D, 6 * D])
    ada_b_d = din("ada_b", [4, 6 * D])
    n1g_d = din("norm1_g", [4, D])
    n2g_d = din("norm2_g", [4, D])
    wqkva_d = din("wqkv_a", [2, D, QKVA])
    qna_d = din("q_norm_a", [2, HD])
    kna_d = din("k_norm_a", [2, HD])
    sinks_d = din("sinks_a", [2, 16])
    woa_d = din("wo_a", [2, D, D])
    wqkvb_d = din("wqkv_b", [2, D, QKVB])
    wob_d = din("wo_b", [2, D, D])
    wg_d = din("w_gate", [4, D, DFF])
    wu_d = din("w_up", [4, D, DFF])
    wd_d = din("w_down", [4, DFF, D])
    out_d = nc.dram_tensor("out", [NSEQ * S_LEN, D], F32, kind="ExternalOutput").ap()

    def dscr(name, shape, dt=BF16):
        return nc.dram_tensor(name, list(shape), dt).ap()

    wqa_img = [dscr("wqa_img%d" % j, [128, 8 * QKVA]) for j in range(2)]
    wqb_img = [dscr("wqb_img%d" % j, [128, 8 * QKVB]) for j in range(2)]
    wg_img = [dscr("wg_img%d" % l, [NFC // 2, 128, 8 * 256]) for l in range(4)]
    wu_img = [dscr("wu_img%d" % l, [NFC // 2, 128, 8 * 256]) for l in range(4)]
    wo_img = [dscr("wo_img%d" % l, [128, 8 * D]) for l in range(4)]
    wd_img = [dscr("wd_img%d" % l, [128, NFC * D]) for l in range(4)]
    MODV = [nc.dram_tensor("modv%d" % l, [NSEQ, 6 * D], F32).ap() for l in range(4)]
    B_modv = [Buf("modv%d" % l) for l in range(4)]
    QT_d = dscr("QT_s", [D, S_LEN])
    KT_d = dscr("KT_s", [D, S_LEN])
    V_d = dscr("V_s", [S_LEN, D])
    AOT_d = dscr("AOT_s", [D, S_LEN])
    QKA_d = dscr("QKA_s", [S_LEN, 1280])
    VA_d = dscr("VA_s", [S_LEN, 256])
    B_QT, B_KT, B_V, B_AOT, B_QKA, B_VA = (Buf(n) for n in ("QT", "KT", "V", "AOT", "QKA", "VA"))
    B_img = {}
    xbuf = [[Buf("x%d_%d" % (b, i)) for i in range(NT)] for b in range(NSEQ)]

    inv_freq = np.power(np.float32(ROPE_THETA), -np.arange(8, dtype=np.float32) * np.float32(2.0) / np.float32(16)).astype(np.float32)

    with ExitStack() as st:
        S = Sched(nc, st)
        AR = Arena(nc, S, SB_BASE, SB_LIMIT)
        psum = st.enter_context(nc.psum_tensor("psum_all", [128, 4096], F32))

        def bank(i, n=1):
            return psum[:, i * 512:(i + n) * 512]

        def bank_bf(i):
            return psum[:, i * 512:(i + 1) * 512].bitcast(BF16)

        ident = AR.alloc("ident", [128, 128], BF16)
        identf = AR.alloc("identf", [128, 128], F32)
        onesb = AR.alloc("onesb", [128, 1024], BF16)
        onesf = AR.alloc("onesf", [128, 128], F32)
        Lmat = AR.alloc("Lmat", [128, 128], BF16)
        tribig = AR.alloc("tribig", [128, 128], BF16)
        negident = AR.alloc("negident", [128, 128], BF16)
        zerob = AR.alloc("zerob", [128, 128], BF16)
        MOD = AR.alloc("MOD", [128, 6, D], F32)
        wo_s = AR.alloc("wo_s", [128, 8, D], BF16)
        wd_s = AR.alloc("wd_s", [128, NFC, D], BF16)
        gain_rep = AR.alloc("gain_rep", [128, 18, HD], F32)
        esink = AR.alloc("esink", [128, 16], F32)
        cs_cos_l = [AR.alloc("cs_cos%d" % i, [128, NT, 8], F32) for i in range(NSEQ)]
        cs_sin_l = [AR.alloc("cs_sin%d" % i, [128, NT, 8], F32) for i in range(NSEQ)]
        invf = AR.alloc("invf", [128, 8], F32)
        PH = AR.mark()

        def mm(out, lhsT, rhs, start, stop, reads, writes, partial):
            S.op("pe", lambda e: e.matmul(out, lhsT=lhsT, rhs=rhs, start=start, stop=stop),
                 reads=reads, writes=writes, partial=partial)

        def tr(out, in_, idn, reads, writes, partial):
            S.op("pe", lambda e: e.transpose(out, in_, idn), reads=reads, writes=writes, partial=partial)

        def act(out, in_, func, reads, writes, partial=False, eng="act", **kw):
            S.op(eng, lambda e: e.activation(out=out, in_=in_, func=func, **kw), reads=reads, writes=writes, partial=partial)

        def tt(eng, out, in0, in1, op, reads, writes, partial=False):
            S.op(eng, lambda e: e.tensor_tensor(out=out, in0=in0, in1=in1, op=op), reads=reads, writes=writes, partial=partial)

        def ts(eng, out, in0, s1, s2, op0, op1, reads, writes, partial=False):
            if s2 is None:
                S.op(eng, lambda e: e.tensor_scalar(out=out, in0=in0, scalar1=s1, scalar2=None, op0=op0), reads=reads, writes=writes, partial=partial)
            else:
                S.op(eng, lambda e: e.tensor_scalar(out=out, in0=in0, scalar1=s1, scalar2=s2, op0=op0, op1=op1), reads=reads, writes=writes, partial=partial)

        def stt(out, in0, scalar, in1, op0, op1, reads, writes, partial=False):
            S.op("dve", lambda e: e.scalar_tensor_tensor(out=out, in0=in0, scalar=scalar, in1=in1, op0=op0, op1=op1),
                 reads=reads, writes=writes, partial=partial)

        def cp(eng, out, in_, reads, writes, partial=False):
            if eng == "act":
                act(out, in_, AF.Copy, reads, writes, partial)
            else:
                S.op(eng, lambda e: e.tensor_copy(out=out, in_=in_), reads=reads, writes=writes, partial=partial)

        def memset(eng, ap, val, writes, partial=False):
            S.op(eng, lambda e: e.memset(ap, val), writes=writes, partial=partial)

        memset("pool", onesb[:], 1.0, [onesb.b])
        memset("pool", onesf[:], 1.0, [onesf.b])
        S.op("pool", lambda e: e.affine_select(out=ident[:], in_=onesb[:, 0:128], pattern=[[-1, 128]], compare_op=ALU.is_equal,
                                               fill=0.0, base=0, channel_multiplier=1), reads=[onesb.b], writes=[ident.b])
        S.op("pool", lambda e: e.affine_select(out=identf[:], in_=onesf[:], pattern=[[-1, 128]], compare_op=ALU.is_equal,
                                               fill=0.0, base=0, channel_multiplier=1), reads=[onesf.b], writes=[identf.b])
        S.op("pool", lambda e: e.affine_select(out=Lmat[:], in_=onesb[:, 0:128], pattern=[[-1, 128]], compare_op=ALU.is_ge,
                                               fill=0.0, base=0, channel_multiplier=1), reads=[onesb.b], writes=[Lmat.b])
        memset("pool", zerob[:], 0.0, [zerob.b])
        ts("pool", negident[:], ident[:], -1.0, None, ALU.mult, None, [ident.b], [negident.b])
        S.op("pool", lambda e: e.affine_select(out=tribig[:], in_=zerob[:], pattern=[[1, 128]], compare_op=ALU.is_gt,
                                               fill=30000.0, base=0, channel_multiplier=-1),
             reads=[zerob.b], writes=[tribig.b])
        for j in range(8):
            memset("pool", invf[:, j:j + 1], float(inv_freq[j]), [invf.b], partial=True)

        def convert_all():
            mk = AR.mark()
            NSL = 4
            stf = [AR.alloc("cvf%d" % i, [128, 8, 256], F32) for i in range(NSL)]
            sth = [AR.alloc("cvh%d" % i, [128, 8, 256], BF16) for i in range(NSL)]
            jobs = []
            for j in range(2):
                if any(l % 2 == 0 and l // 2 == j for l in layers):
                    for c0 in range(0, QKVA, 256):
                        jobs.append((wqkva_d[j], c0, wqa_img[j].rearrange("p (kc n) -> p kc n", kc=8)[:, :, c0:c0 + 256], ("wqa", j)))
                if any(l % 2 == 1 and l // 2 == j for l in layers):
                    for c0 in range(0, QKVB, 256):
                        jobs.append((wqkvb_d[j], c0, wqb_img[j].rearrange("p (kc n) -> p kc n", kc=8)[:, :, c0:c0 + 256], ("wqb", j)))
            for l in layers:
                for g in range(NFC // 2):
                    jobs.append((wg_d[l], g * 256, wg_img[l][g].rearrange("p (kc n) -> p kc n", kc=8), ("wg", l)))
                    jobs.append((wu_d[l], g * 256, wu_img[l][g].rearrange("p (kc n) -> p kc n", kc=8), ("wu", l)))
            for l in layers:
                wo_src = (woa_d if l % 2 == 0 else wob_d)[l // 2]
                for kc in range(8):
                    jobs.append((wo_src[kc * 128:(kc + 1) * 128, :], None, wo_img[l].rearrange("p (kc n) -> p kc n", kc=8)[:, kc, :], ("wo", l)))
                for fc in range(NFC):
                    jobs.append((wd_d[l][fc * 128:(fc + 1) * 128, :], None, wd_img[l].rearrange("p (kc n) -> p kc n", kc=NFC)[:, fc, :], ("wd", l)))
            engs = ["dve", "pool"]
            NJ = len(jobs)

            def aps(n):
                w_ap, c0, dst, key = jobs[n]
                sl = n % NSL
                if c0 is None:
                    return (stf[sl][:].rearrange("p k n -> p (k n)")[:, 0:D], sth[sl][:].rearrange("p k n -> p (k n)")[:, 0:D], w_ap)
                return (stf[sl][:], sth[sl][:], w_ap.rearrange("(kc p) n -> p kc n", p=128)[:, :, c0:c0 + 256])

            def ld(n):
                f_ap, h_ap, src = aps(n)
                sl = n % NSL
                S.dma("sp", f_ap, src, stf[sl].b, writes=[stf[sl].b])

            AHEAD = 2
            for n in range(min(AHEAD, NJ)):
                ld(n)
            for n in range(NJ):
                w_ap, c0, dst, key = jobs[n]
                sl = n % NSL
                if key not in B_img:
                    B_img[key] = Buf("img_%s_%d" % key)
                if n + AHEAD < NJ:
                    ld(n + AHEAD)
                f_ap, h_ap, src = aps(n)
                cp(engs[n % 2], h_ap, f_ap, [stf[sl].b], [sth[sl].b])
                S.dma("act", dst, h_ap, sth[sl].b, reads=[sth[sl].b], writes=[B_img[key]], partial=True)
            S.barrier()
            AR.reset(mk)

        def rope_tables(b):
            mk = AR.mark()
            cs_cos, cs_sin = cs_cos_l[b], cs_sin_l[b]
            pi_t = AR.alloc("pi", [NT, 128], I32)
            pf_t = AR.alloc("pf", [NT, 128], F32)
            posf = AR.alloc("posf", [128, NT], F32)
            ang = AR.alloc("ang", [128, NT, 8], F32)
            kf = AR.alloc("kf", [128, NT, 8], F32)
            ki = AR.alloc("ki", [128, NT, 8], I32)
            rr = AR.alloc("rr", [128, NT, 8], F32)
            m1 = AR.alloc("m1", [128, NT, 8], F32)
            pp = T(bank(0)[:, 0:NT], "pp")
            S.dma("sp", pi_t[:], pos_d[b].rearrange("(n p) -> n p", p=128), pi_t.b, writes=[pi_t.b])
            cp("dve", pf_t[:], pi_t[:], [pi_t.b], [pf_t.b])
            tr(pp.h, pf_t[:], identf[0:NT, 0:NT], [pf_t.b, identf.b], [pp.b], False)
            cp("dve", posf[:], pp.h, [pp.b], [posf.b])
            tt("dve", ang[:], posf[:].unsqueeze(2).to_broadcast([128, NT, 8]), invf[:].unsqueeze(1).to_broadcast([128, NT, 8]), ALU.mult,
               [posf.b, invf.b], [ang.b])
            TWO_PI = 2.0 * np.pi
            C1 = 6.28125
            C2 = TWO_PI - C1

            def reduce_and_sin(dst, shift):
                ts("dve", kf[:], ang[:], 1.0 / TWO_PI, 0.5 + shift / TWO_PI, ALU.mult, ALU.add, [ang.b], [kf.b])
                cp("dve", ki[:], kf[:], [kf.b], [ki.b])
                cp("dve", kf[:], ki[:], [ki.b], [kf.b])
                stt(rr[:], kf[:], -C1, ang[:], ALU.mult, ALU.add, [kf.b, ang.b], [rr.b])
                stt(rr[:], kf[:], -C2, rr[:], ALU.mult, ALU.add, [kf.b, rr.b], [rr.b])
                if shift != 0.0:
                    ts("dve", rr[:], rr[:], shift, None, ALU.add, None, [rr.b], [rr.b])
                ts("dve", m1[:], rr[:], -np.pi, None, ALU.is_lt, None, [rr.b], [m1.b])
                stt(rr[:], m1[:], TWO_PI, rr[:], ALU.mult, ALU.add, [m1.b, rr.b], [rr.b])
                ts("dve", m1[:], rr[:], np.pi, None, ALU.is_gt, None, [rr.b], [m1.b])
                stt(rr[:], m1[:], -TWO_PI, rr[:], ALU.mult, ALU.add, [m1.b, rr.b], [rr.b])
                ts("dve", rr[:], rr[:], np.pi, -np.pi, ALU.min, ALU.max, [rr.b], [rr.b])
                act(dst[:], rr[:], AF.Sin, [rr.b], [dst.b])
            reduce_and_sin(cs_sin, 0.0)
            reduce_and_sin(cs_cos, np.pi / 2)
            S.barrier()
            AR.reset(mk)

        def mod_all():
            mk = AR.mark()
            crow = AR.alloc("crow", [1, NSEQ, D], F32)
            erow = AR.alloc("erow", [1, NSEQ, D], F32)
            condr = AR.alloc("condr", [1, NSEQ, D], F32)
            condT = AR.alloc("condT", [128, 8, NSEQ], F32)
            aw = [AR.alloc("aw%d" % i, [128, 8, 512], F32) for i in range(2)]
            bias = [AR.alloc("bias%d" % i, [NSEQ, 512], F32) for i in range(2)]
            gbc = [AR.alloc("gbc%d" % i, [NSEQ, 512], F32) for i in range(2)]
            tmpm = [AR.alloc("tmpm%d" % i, [NSEQ, 512], F32) for i in range(2)]
            mrow = [AR.alloc("mrow", [NSEQ, 6 * D], F32)] * 2
            pcT = T(bank(0)[:, 0:8 * NSEQ], "pcT")
            pm = [T(bank(2 + i)[0:NSEQ, :], "pm%d" % i) for i in range(2)]
            S.dma("sp", crow[:].rearrange("o b d -> o (b d)"), c_d.rearrange("b d -> (b d)").unsqueeze(0), crow.b, writes=[crow.b])
            act(erow[:], crow[:], AF.Exp, [crow.b], [erow.b], scale=-1.0)
            ts("dve", erow[:], erow[:], 1.0, None, ALU.add, None, [erow.b], [erow.b])
            S.op("dve", lambda e: e.reciprocal(out=erow[:], in_=erow[:]), reads=[erow.b], writes=[erow.b])
            tt("dve", condr[:], crow[:], erow[:], ALU.mult, [crow.b, erow.b], [condr.b])
            first = True
            for kc in range(8):
                for bb in range(NSEQ):
                    mm(pcT.h[:, kc * NSEQ + bb:kc * NSEQ + bb + 1], condr[0:1, bb, kc * 128:(kc + 1) * 128], onesf[0:1, 0:1], True, True,
                       [condr.b, onesf.b], [pcT.b], not first)
                    first = False
            cp("dve", condT[:].rearrange("p k m -> p (k m)"), pcT.h, [pcT.b], [condT.b])
            nj = 0
            for li, L in enumerate(layers):
                awv = ada_w_d[L].rearrange("(kc p) n -> p kc n", p=128)
                mr = mrow[li % 2]
                for j in range(12):
                    v, half = j // 2, j % 2
                    a = aw[nj % 2]
                    S.dma("sp", a[:], awv[:, :, j * 512:(j + 1) * 512], a.b, writes=[a.b])
                    bt = bias[nj % 2]
                    S.dma("sp", bt[:], ada_b_d[L:L + 1, j * 512:(j + 1) * 512].partition_broadcast(NSEQ), bt.b, writes=[bt.b])
                    p = pm[nj % 2]
                    for kc in range(8):
                        mm(p.h, condT[:, kc, :], a[:, kc, :], kc == 0, kc == 7, [condT.b, a.b], [p.b], kc > 0)
                    dst = mr[:, j * 512:(j + 1) * 512]
                    if v in (1, 4):
                        gt = gbc[half]
                        gsrc = n1g_d if v == 1 else n2g_d
                        S.dma("sp", gt[:], gsrc[L:L + 1, half * 512:(half + 1) * 512].partition_broadcast(NSEQ), gt.b, writes=[gt.b])
                        tm = tmpm[half]
                        tt("dve", tm[:], p.h, bt[:], ALU.add, [p.b, bt.b], [tm.b])
                        stt(dst, tm[:], 1.0, gt[:], ALU.add, ALU.mult, [tm.b, gt.b], [mr.b], j > 0)
                    else:
                        tt("dve", dst, p.h, bt[:], ALU.add, [p.b, bt.b], [mr.b], j > 0)
                    nj += 1
                S.dma("sp", MODV[L], mr[:], mr.b, reads=[mr.b], writes=[B_modv[L]])
            S.barrier()
            AR.reset(mk)

        def layer_setup(L):
            isA = (L % 2 == 0)
            Lm = L // 2
            S.dma("sp", wo_s[:].rearrange("p k n -> p (k n)"), wo_img[L], wo_s.b, reads=[B_img[("wo", L)]], writes=[wo_s.b])
            S.dma("sp", wd_s[:].rearrange("p k n -> p (k n)"), wd_img[L], wd_s.b, reads=[B_img[("wd", L)]], writes=[wd_s.b])
            if isA:
                mk = AR.mark()
                gq = AR.alloc("gq", [128, 2, HD], F32)
                sk = AR.alloc("sk", [128, 16], F32)
                S.dma("sp", gq[:, 0, :], qna_d[Lm:Lm + 1, :].partition_broadcast(128), gq.b, writes=[gq.b], partial=True)
                S.dma("sp", gq[:, 1, :], kna_d[Lm:Lm + 1, :].partition_broadcast(128), gq.b, writes=[gq.b], partial=True, chain=True)
                S.dma("sp", sk[:], sinks_d[Lm:Lm + 1, :].partition_broadcast(128), sk.b, writes=[sk.b])
                cp("dve", gain_rep[:, 0:16, :], gq[:, 0:1, :].to_broadcast([128, 16, HD]), [gq.b], [gain_rep.b], True)
                cp("dve", gain_rep[:, 16:18, :], gq[:, 1:2, :].to_broadcast([128, 2, HD]), [gq.b], [gain_rep.b], True)
                act(esink[:], sk[:], AF.Exp, [sk.b], [esink.b])
                S.barrier()
                AR.reset(mk)

        def mod_load(L, b):
            S.dma("sp", MOD[:].rearrange("p v d -> p (v d)"), MODV[L][b:b + 1, :].partition_broadcast(128), MOD.b,
                  reads=[B_modv[L]], writes=[MOD.b])

        def norm_mod_T(xt, slotA, slotB, tmpf, hb, junk, ssq, pT, hT, tcol, first):
            act(hb[:], xt[:], AF.Square, [xt.b], [hb.b, ssq.b], accum_out=ssq[:])
            act(ssq[:], ssq[:], AF.Ln, [ssq.b], [ssq.b], scale=1.0 / D, bias=EPS_T[:])
            act(ssq[:], ssq[:], AF.Exp, [ssq.b], [ssq.b], scale=-0.5)
            stt(tmpf[:], xt[:], ssq[:, 0:1], MOD[:, slotA, :], ALU.mult, ALU.mult, [xt.b, ssq.b, MOD.b], [tmpf.b])
            tt("pool", hb[:], tmpf[:], MOD[:, slotB, :], ALU.add, [tmpf.b, MOD.b], [hb.b])
            pv = pT.h.rearrange("p (k t) -> p k t", t=128)
            for kc in range(8):
                tr(pv[:, kc, :], hb[:, kc * 128:(kc + 1) * 128], ident[:], [hb.b, ident.b], [pT.b], kc > 0)
            cp("act", hT[:, :, tcol * 128:(tcol + 1) * 128], pv, [pT.b], [hT.b], not first)

        def p1_phase(L, b, first_layer):
            mk = AR.mark()
            isA = (L % 2 == 0)
            Lm = L // 2
            NW = QKVA if isA else QKVB
            Wq = AR.alloc("Wq", [128, 8, NW], BF16)
            img = (wqa_img if isA else wqb_img)[Lm]
            ikey = ("wqa" if isA else "wqb", Lm)
            S.dma("sp", Wq[:].rearrange("p k n -> p (k n)"), img, Wq.b, reads=[B_img[ikey]], writes=[Wq.b])
            xts = [AR.alloc("xt%d" % i, [128, D], F32) for i in range(3)]
            tmpf = [AR.alloc("tmpf%d" % i, [128, D], F32) for i in range(2)]
            hb = [AR.alloc("hb%d" % i, [128, D], BF16) for i in range(2)]
            junk = None
            ssq = [AR.alloc("ssq%d" % i, [128, 1], F32) for i in range(2)]
            hT = [AR.alloc("hT%d" % i, [128, 8, 512], BF16) for i in range(2)]
            pT = [T(bank_bf(i), "pT%d" % i) for i in range(2)]
            xsrc = x_d if first_layer else out_d
            if isA:
                pq = [T(bank(2 + 3 * i, 3), "pq%d" % i) for i in range(2)]
                qkf = [AR.alloc("qkf%d" % i, [128, 18, HD], F32) for i in range(2)]
                sqf = AR.alloc("sqf", [128, 18, HD], F32)
                qn = AR.alloc("qn", [128, 18, HD], F32)
                qg = [AR.alloc("qg%d" % i, [128, 18, HD], F32) for i in range(2)]
                s18 = [AR.alloc("s18_%d" % i, [128, 18], F32) for i in range(2)]
                rt = [AR.alloc("rt%d" % i, [128, 18, 8], F32) for i in range(4)]
                qkb = [AR.alloc("qkb%d" % i, [128, 1280], BF16) for i in range(2)]
                vaug = [AR.alloc("vaug%d" % i, [128, 2, 128], BF16) for i in range(2)]
                for v in vaug:
                    memset("pool", v[:], 1.0, [v.b])
            else:
                pb = [T(bank(2 + i), "pb%d" % i) for i in range(6)]
                pv2 = [T(bank(2 + 2 * i, 2), "pv%d" % i) for i in range(3)]
                qst = [AR.alloc("qst%d" % i, [128, 4, 512], BF16) for i in range(2)]
                vst = [AR.alloc("vst%d" % i, [128, D], BF16) for i in range(2)]
            nb = 0
            for c in range(NCH):
                hTc = hT[c % 2]
                for t4 in range(4):
                    i = c * 4 + t4
                    row0 = b * S_LEN + i * 128
                    xt = xts[i % 3]
                    S.dma("sp", xt[:], xsrc[row0:row0 + 128, :], xt.b, reads=([] if first_layer else [xbuf[b][i]]), writes=[xt.b])
                    norm_mod_T(xt, 1, 0, tmpf[i % 2], hb[i % 2], junk, ssq[i % 2], pT[i % 2], hTc, t4, t4 == 0)
                if isA:
                    for t4 in range(4):
                        i = c * 4 + t4
                        p = pq[i % 2]
                        for (c0, c1) in ((0, 512), (512, 1024), (1024, 1280)):
                            for kc in range(8):
                                mm(p.h[:, c0:c1], hTc[:, kc, t4 * 128:(t4 + 1) * 128], Wq[:, kc, c0:c1], kc == 0, kc == 7,
                                   [hTc.b, Wq.b], [p.b], not (c0 == 0 and kc == 0))
                        qf = qkf[i % 2]
                        cp("act", qf[:].rearrange("p h d -> p (h d)"), p.h[:, 0:1152], [p.b], [qf.b])
                        va = vaug[i % 2]
                        cp("dve", va[:, :, 0:64], p.h[:, 1152:1280].rearrange("p (g d) -> p g d", d=64), [p.b], [va.b], True)
                        tt("pool", sqf[:], qf[:], qf[:], ALU.mult, [qf.b], [sqf.b])
                        s1 = s18[i % 2]
                        S.op("dve", lambda e, s1=s1: e.reduce_sum(out=s1[:], in_=sqf[:], axis=AX.X), reads=[sqf.b], writes=[s1.b])
                        act(s1[:], s1[:], AF.Ln, [s1.b], [s1.b], scale=1.0 / HD, bias=EPS_T[:])
                        act(s1[:], s1[:], AF.Exp, [s1.b], [s1.b], scale=-0.5)
                        tt("dve", qn[:], qf[:], s1[:].unsqueeze(2).to_broadcast([128, 18, HD]), ALU.mult, [qf.b, s1.b], [qn.b])
                        g = qg[i % 2]
                        tt("pool", g[:], qn[:], gain_rep[:], ALU.mult, [qn.b, gain_rep.b], [g.b])
                        cs_cos, cs_sin = cs_cos_l[b], cs_sin_l[b]
                        cosb = cs_cos[:, i:i + 1, :].to_broadcast([128, 18, 8])
                        sinb = cs_sin[:, i:i + 1, :].to_broadcast([128, 18, 8])
                        tt("dve", rt[0][:], g[:, :, 0:8], cosb, ALU.mult, [g.b, cs_cos.b], [rt[0].b])
                        tt("dve", rt[1][:], g[:, :, 8:16], sinb, ALU.mult, [g.b, cs_sin.b], [rt[1].b])
                        tt("dve", rt[2][:], g[:, :, 8:16], cosb, ALU.mult, [g.b, cs_cos.b], [rt[2].b])
                        tt("dve", rt[3][:], g[:, :, 0:8], sinb, ALU.mult, [g.b, cs_sin.b], [rt[3].b])
                        tt("dve", g[:, :, 0:8], rt[0][:], rt[1][:], ALU.subtract, [rt[0].b, rt[1].b], [g.b], True)
                        tt("dve", g[:, :, 8:16], rt[2][:], rt[3][:], ALU.add, [rt[2].b, rt[3].b], [g.b], True)
                        qb = qkb[i % 2]
                        cp("act", qb[:, 0:1024], g[:, 0:16, :].rearrange("p h d -> p (h d)"), [g.b], [qb.b])
                        kv4 = qb[:, 1024:1280].rearrange("p (g u d) -> p g u d", g=2, u=2)
                        for u in range(2):
                            cp("pool", kv4[:, :, u, :], g[:, 16:18, :], [g.b], [qb.b], True)
                        S.dma("act", QKA_d[i * 128:(i + 1) * 128, :], qb[:], qb.b, reads=[qb.b], writes=[B_QKA], partial=True)
                        S.dma("act", VA_d[i * 128:(i + 1) * 128, :], va[:].rearrange("p g d -> p (g d)"), va.b, reads=[va.b], writes=[B_VA], partial=True)
                else:
                    for m in range(16):
                        p = pb[nb % 6]
                        nb += 1
                        for kc in range(8):
                            mm(p.h, Wq[:, kc, m * 128:(m + 1) * 128], hTc[:, kc, :], kc == 0, kc == 7, [Wq.b, hTc.b], [p.b], kc > 0)
                        qs = qst[(m // 4) % 2]
                        if m >= 8:
                            cp("act" if m % 2 else "dve", qs[:, m % 4, :], p.h, [p.b], [qs.b], (m % 4) != 0)
                        elif m % 2:
                            act(qs[:, m % 4, :], p.h, AF.Copy, [p.b], [qs.b], (m % 4) != 0, scale=0.125)
                        else:
                            ts("dve", qs[:, m % 4, :], p.h, 0.125, None, ALU.mult, None, [p.b], [qs.b], (m % 4) != 0)
                        if m % 4 == 3:
                            m0 = m - 3
                            dst_t, dst_b = (QT_d, B_QT) if m0 < 8 else (KT_d, B_KT)
                            r0 = (m0 % 8) * 128
                            S.dma("act", dst_t[r0:r0 + 512, c * 512:(c + 1) * 512].rearrange("(m p) n -> p m n", p=128), qs[:], qs.b,
                                  reads=[qs.b], writes=[dst_b], partial=True)
                    for t4 in range(4):
                        i = c * 4 + t4
                        p = pv2[i % 3]
                        pbs = [pb[2 * (i % 3)].b, pb[2 * (i % 3) + 1].b]
                        for nh in range(2):
                            for kc in range(8):
                                mm(p.h[:, nh * 512:(nh + 1) * 512], hTc[:, kc, t4 * 128:(t4 + 1) * 128], Wq[:, kc, 2048 + nh * 512:2048 + (nh + 1) * 512],
                                   kc == 0, kc == 7, [hTc.b, Wq.b], [pbs[nh]], kc > 0)
                        vs = vst[i % 2]
                        cp("dve" if i % 2 else "act", vs[:], p.h, pbs, [vs.b])
                        S.dma("act", V_d[i * 128:(i + 1) * 128, :], vs[:], vs.b, reads=[vs.b], writes=[B_V], partial=True)
            S.barrier()
            AR.reset(mk)

        def p2a_phase(L, b):
            mk = AR.mark()
            qin = [AR.alloc("qin%d" % i, [128, 1280], BF16) for i in range(2)]
            vaug = [AR.alloc("vaug%d" % i, [128, 2, 128], BF16) for i in range(2)]
            qT = [AR.alloc("qT%d" % i, [128, 8, 128], BF16) for i in range(2)]
            kT = [AR.alloc("kT%d" % i, [128, 2, 128], BF16) for i in range(2)]
            Pm = [AR.alloc("Pm%d" % i, [128, 2048], BF16) for i in range(2)]
            AO = [AR.alloc("AO%d" % i, [128, D], BF16) for i in range(2)]
            den = [AR.alloc("den%d" % i, [128, 8, 1], F32) for i in range(2)]
            aoT = [AR.alloc("aoT%d" % i, [128, 8, 512], BF16) for i in range(2)]
            amask = AR.alloc("amask", [128, 2048], BF16)
            am4 = amask[:].rearrange("p (a k t) -> p a k t", k=2, t=128)
            on3 = onesb[:].rearrange("p (a t) -> p a t", t=128)
            S.op("pool", lambda e: e.affine_select(out=am4[:, :, 1, :], in_=on3, pattern=[[0, 8], [1, 128]], compare_op=ALU.is_ge,
                                                   fill=0.0, base=0, channel_multiplier=-1), reads=[onesb.b], writes=[amask.b], partial=True)
            S.op("pool", lambda e: e.affine_select(out=am4[:, :, 0, :], in_=on3, pattern=[[0, 8], [-1, 128]], compare_op=ALU.is_gt,
                                                   fill=0.0, base=0, channel_multiplier=1), reads=[onesb.b], writes=[amask.b], partial=True)
            pTq = T(bank_bf(0), "pTq")
            pTk = T(bank_bf(1), "pTk")
            pS = T(bank(2, 4), "pS")
            pO = T(bank(6, 2), "pO")
            memset("pool", kT[1][:], 0.0, [kT[1].b])
            memset("pool", vaug[1][:], 0.0, [vaug[1].b])
            SCALE = HD ** -0.5
            for i in range(NT):
                c, t4 = i // 4, i % 4
                cur, prv = i % 2, (i - 1) % 2
                qi = qin[cur]
                va = vaug[cur]
                S.dma("sp", qi[:], QKA_d[i * 128:(i + 1) * 128, :], qi.b, reads=[B_QKA], writes=[qi.b])
                S.dma("sp", va[:].rearrange("p g d -> p (g d)"), VA_d[i * 128:(i + 1) * 128, :], va.b, reads=[B_VA], writes=[va.b])
                pq3 = pTq.h.rearrange("p (k t) -> p k t", t=128)
                pk3 = pTk.h.rearrange("p (k t) -> p k t", t=128)
                for kc in range(8):
                    tr(pq3[:, kc, :], qi[:, kc * 128:(kc + 1) * 128], ident[:], [qi.b, ident.b], [pTq.b], kc > 0)
                for g in range(2):
                    tr(pk3[:, g, :], qi[:, 1024 + g * 128:1024 + (g + 1) * 128], ident[:], [qi.b, ident.b], [pTk.b], g > 0)
                cp("dve", qT[cur][:], pq3, [pTq.b], [qT[cur].b])
                cp("act", kT[cur][:], pk3[:, 0:2, :], [pTk.b], [kT[cur].b])
                ao = AO[cur]
                for g in range(2):
                    first = True
                    for par in range(2):
                        for cq in range(4):
                            for kt in range(2):
                                col = par * 1024 + cq * 256 + kt * 128
                                kbuf = kT[prv] if kt == 0 else kT[cur]
                                mm(pS.h[:, col:col + 128], kbuf[par * 64:(par + 1) * 64, g, :], qT[cur][par * 64:(par + 1) * 64, 4 * g + cq, :],
                                   True, True, [kbuf.b, qT[cur].b], [pS.b], not first)
                                first = False
                    P = Pm[g]
                    act(P[:], pS.h, AF.Exp, [pS.b], [P.b], scale=SCALE)
                    tt("dve", P[:], P[:], amask[:], ALU.mult, [P.b, amask.b], [P.b])
                    po3 = pO.h.rearrange("p (h d) -> p h d", d=128)
                    first = True
                    for cq in range(4):
                        for par in range(2):
                            hh = 2 * cq + par
                            col = par * 1024 + cq * 256
                            mm(po3[:, hh, :], P[:, col:col + 128], vaug[prv][:, g, :], True, False, [P.b, vaug[prv].b], [pO.b], not first)
                            first = False
                            mm(po3[:, hh, :], P[:, col + 128:col + 256], vaug[cur][:, g, :], False, True, [P.b, vaug[cur].b], [pO.b], True)
                    dn = den[g]
                    tt("dve", dn[:], po3[:, :, 64:65], esink[:, 8 * g:8 * g + 8].unsqueeze(2), ALU.add, [pO.b, esink.b], [dn.b])
                    S.op("dve", lambda e, dn=dn: e.reciprocal(out=dn[:], in_=dn[:]), reads=[dn.b], writes=[dn.b])
                    tt("dve", ao[:, g * 512:(g + 1) * 512].rearrange("p (h d) -> p h d", d=64), po3[:, :, 0:64], dn[:].to_broadcast([128, 8, 64]),
                       ALU.mult, [pO.b, dn.b], [ao.b], g > 0)
                for kc in range(8):
                    tr(pq3[:, kc, :], ao[:, kc * 128:(kc + 1) * 128], ident[:], [ao.b, ident.b], [pTq.b], kc > 0)
                at = aoT[c % 2]
                cp("act", at[:, :, t4 * 128:(t4 + 1) * 128], pq3, [pTq.b], [at.b], t4 > 0)
                if t4 == 3:
                    S.dma("sp", AOT_d[:, c * 512:(c + 1) * 512].rearrange("(k p) n -> p k n", p=128), at[:], at.b, reads=[at.b], writes=[B_AOT], partial=True)
            S.barrier()
            AR.reset(mk)

        def p2b_phase(L, b):
            mk = AR.mark()
            Q0 = [AR.alloc("Q0_%d" % i, [128, S_LEN], BF16) for i in range(2)]
            Q1 = [AR.alloc("Q1_%d" % i, [128, S_LEN], BF16) for i in range(2)]
            KTt = [AR.alloc("KTt%d" % i, [128, S_LEN], BF16) for i in range(2)]
            KTn = [AR.alloc("KTn%d" % i, [128, S_LEN], BF16) for i in range(2)]
            Vt = [AR.alloc("Vt%d" % i, [128, NT, 128], BF16) for i in range(2)]
            Ef = [AR.alloc("Ef%d" % i, [128, 1024], F32) for i in range(2)]
            SP = [AR.alloc("SP%d" % i, [128, 1024], BF16) for i in range(2)]
            Ab = [AR.alloc("Ab%d" % i, [128, 1024], BF16) for i in range(2)]
            Srun = [AR.alloc("Srun%d" % i, [128, 1024], BF16) for i in range(2)]
            ost = [AR.alloc("ost%d" % i, [128, 512], BF16) for i in range(2)]
            pZ = T(bank(0, 2), "pZ")
            pD = [T(bank(2 + 2 * i, 2), "pD%d" % i) for i in range(2)]
            pOT = T(bank(6, 2), "pOT")
            for i in range(2):
                memset("pool", Q0[i][64:128, :], 0.0, [Q0[i].b])
                memset("pool", Q1[i][0:64, :], 0.0, [Q1[i].b])
            its = []
            ng = 0
            for hp in range(8):
                for qc in range(NCH):
                    kbs = list(range(4 * qc + 3, -1, -1))
                    for n, kb in enumerate(kbs):
                        its.append(dict(hp=hp, qc=qc, kb=kb, first=(n == 0), last=(kb == 0), grp=ng, idx=len(its), k=n))
                    ng += 1
            loaded = set()

            def load_pair(hp):
                if hp in loaded or hp >= 8:
                    return
                loaded.add(hp)
                s = hp % 2
                S.dma("sp", Q0[s][0:64, :], QT_d[hp * 128:hp * 128 + 64, :], Q0[s].b, reads=[B_QT], writes=[Q0[s].b], partial=True)
                S.dma("sp", Q1[s][64:128, :], QT_d[hp * 128 + 64:hp * 128 + 128, :], Q1[s].b, reads=[B_QT], writes=[Q1[s].b], partial=True)
                S.dma("sp", KTt[s][:], KT_d[hp * 128:(hp + 1) * 128, :], KTt[s].b, reads=[B_KT], writes=[KTt[s].b])
                S.dma("sp", Vt[s][:], V_d[:, hp * 128:(hp + 1) * 128].rearrange("(k p) d -> p k d", p=128), Vt[s].b, reads=[B_V], writes=[Vt[s].b])
                ts("dve", KTn[s][:], KTt[s][:], -1.0, None, ALU.mult, None, [KTt[s].b], [KTn[s].b])

            def cols(it, par):
                r = it["kb"] - 4 * it["qc"]
                c0 = 128 * r if r >= 0 else 0
                return c0, slice(par * 512 + c0, (par + 1) * 512)

            def v3(ap, it):
                r = it["kb"] - 4 * it["qc"]
                if r <= 0:
                    return ap
                return ap.rearrange("p (a f) -> p a f", a=2)[:, :, 128 * r:512]

            def zmm(it):
                hp, qc, kb, n = it["hp"], it["qc"], it["kb"], it["idx"]
                load_pair(hp)
                s = hp % 2
                r = kb - 4 * qc
                Qp = (Q0[s], Q1[s])
                for par in range(2):
                    c0, cs = cols(it, par)
                    mm(pZ.h[:, cs], KTt[s][:, kb * 128:(kb + 1) * 128], Qp[par][:, qc * 512 + c0:(qc + 1) * 512], True, r < 0,
                       [KTt[s].b, Qp[par].b], [pZ.b], par > 0)
                    if r >= 0:
                        mm(pZ.h[:, par * 512 + c0:par * 512 + c0 + 128], negident[:], tribig[:], False, True,
                           [negident.b, tribig.b], [pZ.b], True)

            def es(it):
                n = it["idx"]
                ef = Ef[n % 2]
                act(v3(ef[:], it), v3(pZ.h, it), AF.Exp, [pZ.b], [ef.b])
                sp = SP[n % 2]
                act(v3(sp[:], it), v3(ef[:], it), AF.Ln, [ef.b], [sp.b], bias=ONE_T[:])

            def dmm(it):
                hp, qc, kb, n, k = it["hp"], it["qc"], it["kb"], it["idx"], it["k"]
                s = hp % 2
                r = kb - 4 * qc
                Qp = (Q0[s], Q1[s])
                sp = SP[n % 2]
                d = pD[n % 2]
                sr_prev = Srun[(k - 1) % 2]
                for par in range(2):
                    c0, cs = cols(it, par)
                    mm(d.h[:, cs], Lmat[:], sp[:, cs], True, False, [Lmat.b, sp.b], [d.b], par > 0)
                    if not it["first"]:
                        mm(d.h[:, cs], onesb[:, 0:128], sr_prev[:, cs], False, False, [onesb.b, sr_prev.b], [d.b], True)
                    mm(d.h[:, cs], KTn[s][:, kb * 128:(kb + 1) * 128], Qp[par][:, qc * 512 + c0:(qc + 1) * 512], False, r < 0,
                       [KTn[s].b, Qp[par].b], [d.b], True)
                    if r >= 0:
                        mm(d.h[:, par * 512 + c0:par * 512 + c0 + 128], ident[:], tribig[:], False, True, [ident.b, tribig.b], [d.b], True)

            def aexp(it):
                n = it["idx"]
                act(v3(Ab[n % 2][:], it), v3(pD[n % 2].h, it), AF.Exp, [pD[n % 2].b], [Ab[n % 2].b], scale=-1.0)

            def sadd(it):
                n, k = it["idx"], it["k"]
                sp = SP[n % 2]
                sr_prev = Srun[(k - 1) % 2]
                sr_new = Srun[k % 2]
                if it["first"]:
                    memset("pool", sr_prev[:], 0.0, [sr_prev.b])
                    memset("pool", sr_new[:], 0.0, [sr_new.b])
                if not it["last"]:
                    if it["first"]:
                        cp("dve", v3(sr_new[:], it), v3(sp[:], it), [sp.b], [sr_new.b], True)
                    else:
                        tt("dve", v3(sr_new[:], it), v3(sr_prev[:], it), v3(sp[:], it), ALU.add, [sr_prev.b, sp.b], [sr_new.b])

            def pv(it):
                hp, qc, kb, n = it["hp"], it["qc"], it["kb"], it["idx"]
                s = hp % 2
                ab = Ab[n % 2]
                if it["first"]:
                    for par in range(2):
                        mm(pOT.h[:, par * 512:(par + 1) * 512], zerob[:], Q0[s][:, qc * 512:(qc + 1) * 512], True, False,
                           [zerob.b, Q0[s].b], [pOT.b], par > 0)
                for par in range(2):
                    c0, cs = cols(it, par)
                    mm(pOT.h[:, cs], Vt[s][:, kb, :], ab[:, cs], False, it["last"], [Vt[s].b, ab.b], [pOT.b], True)
                if it["last"]:
                    o = ost[it["grp"] % 2]
                    cp("dve", o[0:64, :], pOT.h[0:64, 0:512], [pOT.b], [o.b])
                    cp("dve", o[64:128, :], pOT.h[64:128, 512:1024], [pOT.b], [o.b], True)
                    S.dma("sp", AOT_d[hp * 128:(hp + 1) * 128, qc * 512:(qc + 1) * 512], o[:], o.b, reads=[o.b], writes=[B_AOT], partial=True)
                    if qc == 0:
                        load_pair(hp + 1)

            NI = len(its)
            load_pair(0)
            zmm(its[0])
            es(its[0])
            if NI > 1:
                zmm(its[1])
            for n in range(NI):
                if n + 1 < NI:
                    es(its[n + 1])
                dmm(its[n])
                if n >= 1:
                    pv(its[n - 1])
                if n + 2 < NI:
                    zmm(its[n + 2])
                aexp(its[n])
                sadd(its[n])
            pv(its[NI - 1])
            S.barrier()
            AR.reset(mk)

        def p3_phase(L, b, first_layer):
            mk = AR.mark()
            xts = [AR.alloc("xt%d" % i, [128, D], F32) for i in range(5)]
            tmpf = AR.alloc("tmpf", [128, D], F32)
            hb = [AR.alloc("hb%d" % i, [128, D], BF16) for i in range(2)]
            junk = None
            tmpg = AR.alloc("tmpg", [128, D], F32)
            ssq = [AR.alloc("ssq%d" % i, [128, 1], F32) for i in range(2)]
            hT = AR.alloc("hT", [128, 8, 512], BF16)
            aoT = [AR.alloc("aoT%d" % i, [128, 8, 512], BF16) for i in range(2)]
            actT = AR.alloc("actT", [128, NFC, 512], BF16)
            wgu = [AR.alloc("wgu%d" % i, [128, 2, 8 * 256], BF16) for i in range(2)]
            ef = [AR.alloc("ef%d" % i, [128, 512], F32) for i in range(2)]
            tf = [AR.alloc("tf%d" % i, [128, 512], F32) for i in range(2)]
            pY = T(bank(0, 2), "pY")
            pT = [T(bank_bf(2), "pT0"), T(bank_bf(7), "pT1")]
            pG = [T(bank(3 + 2 * i), "pG%d" % i) for i in range(2)]
            pU = [T(bank(4 + 2 * i), "pU%d" % i) for i in range(2)]
            pW = T(bank(3, 2), "pW")
            pW_bufs = [pG[0].b, pU[0].b]
            xsrc = x_d if first_layer else out_d
            st = dict(nx=0, nf=0)
            xl = {}

            def load_ao(c):
                at = aoT[c % 2]
                S.dma("sp", at[:], AOT_d[:, c * 512:(c + 1) * 512].rearrange("(k p) n -> p k n", p=128), at.b, reads=[B_AOT], writes=[at.b])

            def load_x(c, t4):
                i = c * 4 + t4
                row0 = b * S_LEN + i * 128
                xt = xts[st["nx"] % 5]
                st["nx"] += 1
                xl[(c, t4)] = xt
                S.dma("sp", xt[:], xsrc[row0:row0 + 128, :], xt.b, reads=([] if first_layer else [xbuf[b][i]]), writes=[xt.b])

            def front_wo(c, t4):
                i = c * 4 + t4
                at = aoT[c % 2]
                xt = xl[(c, t4)]
                for nh in range(2):
                    for kc in range(8):
                        mm(pW.h[:, nh * 512:(nh + 1) * 512], at[:, kc, t4 * 128:(t4 + 1) * 128], wo_s[:, kc, nh * 512:(nh + 1) * 512],
                           kc == 0, kc == 7, [at.b, wo_s.b], [pW_bufs[nh]], kc > 0)
                tt("dve", tmpg[:], pW.h, MOD[:, 2, :], ALU.mult, pW_bufs + [MOD.b], [tmpg.b])
                tt("pool", xt[:], tmpg[:], xt[:], ALU.add, [tmpg.b, xt.b], [xt.b])
                hbt = hb[i % 2]
                sq = ssq[i % 2]
                act(hbt[:], xt[:], AF.Square, [xt.b], [hbt.b, sq.b], accum_out=sq[:])
                act(sq[:], sq[:], AF.Ln, [sq.b], [sq.b], scale=1.0 / D, bias=EPS_T[:])
                act(sq[:], sq[:], AF.Exp, [sq.b], [sq.b], scale=-0.5)
                stt(tmpf[:], xt[:], sq[:, 0:1], MOD[:, 4, :], ALU.mult, ALU.mult, [xt.b, sq.b, MOD.b], [tmpf.b])
                tt("pool", hbt[:], tmpf[:], MOD[:, 3, :], ALU.add, [tmpf.b, MOD.b], [hbt.b])

            def front_T(c, t4):
                i = c * 4 + t4
                hbt = hb[i % 2]
                p = pT[i % 2]
                pv = p.h.rearrange("p (k t) -> p k t", t=128)
                for kc in range(8):
                    tr(pv[:, kc, :], hbt[:, kc * 128:(kc + 1) * 128], ident[:], [hbt.b, ident.b], [p.b], kc > 0)
                cp("act", hT[:, :, t4 * 128:(t4 + 1) * 128], pv, [p.b], [hT.b], t4 > 0)

            def gateup(c):
                for fg in range(NFC // 2):
                    w = wgu[fg % 2]
                    S.dma("sp", w[:, 0, :], wg_img[L][fg], w.b, reads=[B_img[("wg", L)]], writes=[w.b], partial=False)
                    S.dma("sp", w[:, 1, :], wu_img[L][fg], w.b, reads=[B_img[("wu", L)]], writes=[w.b], partial=True, chain=True)
                    if fg == 1 and c + 1 < NCH:
                        load_ao(c + 1)
                        load_x(c + 1, 0)
                    wg3 = w[:, 0, :].rearrange("p (k n) -> p k n", k=8)
                    wu3 = w[:, 1, :].rearrange("p (k n) -> p k n", k=8)
                    for j in range(2):
                        f = fg * 2 + j
                        nf = st["nf"]
                        g_, u_ = pG[nf % 2], pU[nf % 2]
                        e_, t_ = ef[nf % 2], tf[nf % 2]
                        st["nf"] += 1
                        for kc in range(8):
                            mm(g_.h, wg3[:, kc, j * 128:(j + 1) * 128], hT[:, kc, :], kc == 0, kc == 7, [w.b, hT.b], [g_.b], kc > 0)
                        for kc in range(8):
                            mm(u_.h, wu3[:, kc, j * 128:(j + 1) * 128], hT[:, kc, :], kc == 0, kc == 7, [w.b, hT.b], [u_.b], kc > 0)
                        act(e_[:], g_.h, AF.Exp, [g_.b], [e_.b], scale=-1.0)
                        act(e_[:], e_[:], AF.Ln, [e_.b], [e_.b], bias=ONE_T[:])
                        act(e_[:], e_[:], AF.Exp, [e_.b], [e_.b], scale=-1.0)
                        tt("dve", t_[:], g_.h, e_[:], ALU.mult, [g_.b, e_.b], [t_.b])
                        tt("dve", actT[:, f, :], u_.h, t_[:], ALU.mult, [u_.b, t_.b], [actT.b], f > 0)

            def down(c, t4):
                i = c * 4 + t4
                row0 = b * S_LEN + i * 128
                xt = xl[(c, t4)]
                for nh in range(2):
                    for fc in range(NFC):
                        mm(pY.h[:, nh * 512:(nh + 1) * 512], actT[:, fc, t4 * 128:(t4 + 1) * 128], wd_s[:, fc, nh * 512:(nh + 1) * 512],
                           fc == 0, fc == NFC - 1, [actT.b, wd_s.b], [pY.b], not (nh == 0 and fc == 0))
                tt("dve", tmpg[:], pY.h, MOD[:, 5, :], ALU.mult, [pY.b, MOD.b], [tmpg.b])
                tt("pool", xt[:], tmpg[:], xt[:], ALU.add, [tmpg.b, xt.b], [xt.b])
                S.dma("act", out_d[row0:row0 + 128, :], xt[:], xt.b, reads=[xt.b], writes=[xbuf[b][i]])

            load_ao(0)
            for t4 in range(4):
                load_x(0, t4)
            for t4 in range(4):
                front_wo(0, t4)
                front_T(0, t4)
            for c in range(NCH):
                gateup(c)
                nxt = c + 1 < NCH
                for t4 in range(4):
                    down(c, t4)
                    if nxt:
                        if t4 >= 1:
                            front_T(c + 1, t4 - 1)
                        if t4 >= 1:
                            load_x(c + 1, t4)
                        front_wo(c + 1, t4)
                if nxt:
                    front_T(c + 1, 3)
            S.barrier()
            AR.reset(mk)

        EPS_T = AR.alloc("eps_t", [128, 1], F32)
        ONE_T = AR.alloc("one_t", [128, 1], F32)
        memset("pool", EPS_T[:], EPS, [EPS_T.b])
        memset("pool", ONE_T[:], 1.0, [ONE_T.b])
        PH = AR.mark()
        if do_prologue:
            convert_all()
        mod_all()
        if any(L % 2 == 0 for L in layers):
            for b in range(NSEQ):
                rope_tables(b)
        for li, L in enumerate(layers):
            layer_setup(L)
            for b in range(NSEQ):
                mod_load(L, b)
                p1_phase(L, b, li == 0)
                if L % 2 == 0:
                    p2a_phase(L, b)
                else:
                    p2b_phase(L, b)
                p3_phase(L, b, li == 0)
        S.emit()
        build.ninst = S.ninst
        build.nsem = len(S.sems)
    return nc


_NC_CACHE = {}


def kernel(x, c, positions, ada_w, ada_b, norm1_g, norm2_g, wqkv_a, q_norm_a, k_norm_a, sinks_a, wo_a,
           wqkv_b, wo_b, w_gate, w_up, w_down):
    f = lambda a: np.ascontiguousarray(np.asarray(a, dtype=np.float32))
    x = f(x)
    B, S_, D_ = x.shape
    nseq = B // N_CORES
    if "nc" not in _NC_CACHE:
        _NC_CACHE["nc"] = build(S_, nseq, (0, 1, 2, 3))
    nc = _NC_CACHE["nc"]
    shared = dict(ada_w=f(ada_w), ada_b=f(ada_b), norm1_g=f(norm1_g), norm2_g=f(norm2_g), wqkv_a=f(wqkv_a),
                  q_norm_a=f(q_norm_a), k_norm_a=f(k_norm_a), sinks_a=f(sinks_a), wo_a=f(wo_a), wqkv_b=f(wqkv_b),
                  wo_b=f(wo_b), w_gate=f(w_gate), w_up=f(w_up), w_down=f(w_down))
    c = f(c)
    positions = np.ascontiguousarray(np.asarray(positions, dtype=np.int32))
    in_maps = []
    for i in range(N_CORES):
        m = dict(shared)
        m["x"] = x[i * nseq:(i + 1) * nseq].reshape(nseq * S_, D_)
        m["c"] = c[i * nseq:(i + 1) * nseq]
        m["positions"] = positions[i * nseq:(i + 1) * nseq]
        in_maps.append(m)
    res = run_bass_kernel_spmd(nc, in_maps, core_ids=list(range(N_CORES)))
    outs = [np.asarray(r["out"], dtype=np.float32).reshape(nseq, S_, D_) for r in res.results]
    return np.concatenate(outs, axis=0)
```

```python
import numpy as np
import ml_dtypes
from contextlib import ExitStack
import concourse.bass as bass
import concourse.mybir as mybir
from concourse.bass_utils import run_bass_kernel_spmd

F32 = mybir.dt.float32
BF16 = mybir.dt.bfloat16
I32 = mybir.dt.int32
AF = mybir.ActivationFunctionType
ALU = mybir.AluOpType
AX = mybir.AxisListType

D = 1024
DFF = 2816
NFC = DFF // 128
HD = 64
EPS = 1e-6
QKVA = 1280
QKVB = 3072
N_CORES = 8
SEQ = 4096
BATCH = 16
ROPE_THETA = 500000.0

SB_BASE = 16512
SB_LIMIT = 229344
DEBUG_ARENA = False


class Buf:
    __slots__ = ("name", "w", "r", "wf", "dsem", "dcnt")

    def __init__(self, name):
        self.name = name
        self.w = {}
        self.r = {}
        self.wf = {}
        self.dsem = None
        self.dcnt = 0


class T:
    __slots__ = ("h", "b")

    def __init__(self, h, name):
        self.h = h
        self.b = Buf(name)

    def __getitem__(self, k):
        return self.h[k]


class Sched:
    ENGS = ("pe", "act", "dve", "pool", "sp")

    def __init__(self, nc, stack):
        self.nc = nc
        self.stack = stack
        self.sems = {}
        self.owner = {}
        self.dmax = {}
        self.free = []
        self.cnt = {}
        self.prog = {e: [] for e in self.ENGS}
        self.seen = {e: {} for e in self.ENGS}
        for e in self.ENGS:
            k = "E_" + e
            self.sems[k] = stack.enter_context(nc.semaphore("sem_" + e))
            self.cnt[e] = 0
        self.nd = 0
        self.ninst = 0

    def _dsem(self, b):
        if b.dsem is None:
            if self.free:
                k = self.free.pop()
                b.dcnt = self.dmax[k]
            else:
                k = "D%d" % self.nd
                self.nd += 1
                self.sems[k] = self.stack.enter_context(self.nc.semaphore("d%d" % self.nd))
                self.dmax[k] = 0
            self.owner[k] = b
            b.dsem = k
        return b.dsem

    def release(self, b):
        if b.dsem is not None:
            self.free.append(b.dsem)
            b.dsem = None

    def _need(self, eng, reads, writes, partial):
        need = {}
        for b in reads:
            for k, v in b.w.items():
                if need.get(k, 0) < v:
                    need[k] = v
        for b in writes:
            for k, v in b.r.items():
                if need.get(k, 0) < v:
                    need[k] = v
            for k, v in (b.wf if partial else b.w).items():
                if need.get(k, 0) < v:
                    need[k] = v
        out = []
        seen = self.seen[eng]
        for k, v in need.items():
            if k in self.dmax:
                v = max(v, self.dmax[k])
            if eng == "pe" and k == "E_pe":
                continue
            if seen.get(k, 0) >= v:
                continue
            seen[k] = v
            out.append((k, v))
        return out

    def _commit(self, key, val, reads, writes, partial):
        for b in reads:
            if b.r.get(key, 0) < val:
                b.r[key] = val
        for b in writes:
            if partial:
                if b.w.get(key, 0) < val:
                    b.w[key] = val
            else:
                b.w = {key: val}
                b.wf = {key: val}
                b.r = {}

    def op(self, eng, fn, reads=(), writes=(), partial=False):
        waits = self._need(eng, reads, writes, partial)
        self.cnt[eng] += 1
        key = "E_" + eng
        self.prog[eng].append((waits, fn, key, 1))
        self._commit(key, self.cnt[eng], reads, writes, partial)
        self.ninst += 1 + len(waits)

    def dma(self, q, out, in_, sb, reads=(), writes=(), partial=False, chain=False, **kw):
        key = self._dsem(sb)
        waits = self._need(q, reads, writes, partial)
        if not chain and self.seen[q].get(key, 0) < sb.dcnt:
            self.seen[q][key] = sb.dcnt
            waits = [w for w in waits if w[0] != key] + [(key, sb.dcnt)]
        sb.dcnt += 16
        self.dmax[key] = sb.dcnt
        self.prog[q].append((waits, (lambda e: e.dma_start(out=out, in_=in_, **kw)), key, 16))
        self._commit(key, sb.dcnt, reads, writes, partial)
        self.ninst += 1 + len(waits)

    def barrier(self):
        for e in self.ENGS:
            waits = []
            seen = self.seen[e]
            for k in self.sems:
                if k.startswith("E_"):
                    tgt = self.cnt[k[2:]]
                else:
                    tgt = self.dmax[k]
                if k == "E_" + e and e == "pe":
                    pass
                if seen.get(k, 0) < tgt:
                    seen[k] = tgt
                    waits.append((k, tgt))
            if waits:
                self.prog[e].append((waits, None, None, 0))
                self.ninst += len(waits)

    def emit(self):
        nc = self.nc
        engmap = {"pe": "tensor", "act": "scalar", "dve": "vector", "pool": "gpsimd", "sp": "sync"}
        with nc.Block() as block:
            for e in self.ENGS:
                prog = self.prog[e]

                def body(eng, prog=prog):
                    for waits, fn, key, inc in prog:
                        for k, v in waits:
                            eng.wait_ge(self.sems[k], v)
                        if fn is not None:
                            fn(eng).then_inc(self.sems[key], inc)
                getattr(block, engmap[e])(body)


class Arena:
    def __init__(self, nc, S, base, limit):
        self.nc = nc
        self.S = S
        self.top = base
        self.limit = limit
        self.tiles = []
        self.n = 0

    def alloc(self, name, shape, dt):
        isz = 4 if dt in (F32, I32) else 2
        n = 1
        for s in shape[1:]:
            n *= s
        size = (n * isz + 31) // 32 * 32
        assert self.top + size <= self.limit, ("SBUF overflow", name, self.top, size, self.limit)
        self.n += 1
        h = self.nc.alloc_sbuf_tensor_at("%s_%d" % (name, self.n), list(shape), dt, offset=self.top)
        self.top += size
        t = T(h, name)
        self.tiles.append(t)
        return t

    def mark(self):
        return (self.top, len(self.tiles))

    def reset(self, mk):
        self.peak = max(getattr(self, "peak", 0), self.top)
        if DEBUG_ARENA:
            print("arena reset: top", self.top, "limit", self.limit, "free", self.limit - self.top)
        top, n = mk
        for t in self.tiles[n:]:
            self.S.release(t.b)
        del self.tiles[n:]
        self.top = top


def build(S_LEN=SEQ, NSEQ=2, layers=(0, 1, 2, 3), do_prologue=True):
    NT = S_LEN // 128
    NCH = S_LEN // 512
    nc = bass.Bass("TRN2", target_bir_lowering=False)

    def din(name, shape, dt=F32):
        return nc.dram_tensor(name, list(shape), dt, kind="ExternalInput").ap()

    x_d = din("x", [NSEQ * S_LEN, D])
    c_d = din("c", [NSEQ, D])
    pos_d = din("positions", [NSEQ, S_LEN], I32)
    ada_w_d = din("ada_w", [4, D, 6 * D])
    ada_b_d = din("ada_b", [4, 6 * D])
    n1g_d = din("norm1_g", [4, D])
    n2g_d = din("norm2_g", [4, D])
    wqkva_d = din("wqkv_a", [2, D, QKVA])
    qna_d = din("q_norm_a", [2, HD])
    kna_d = din("k_norm_a", [2, HD])
    sinks_d = din("sinks_a", [2, 16])
    woa_d = din("wo_a", [2, D, D])
    wqkvb_d = din("wqkv_b", [2, D, QKVB])
    wob_d = din("wo_b", [2, D, D])
    wg_d = din("w_gate", [4, D, DFF])
    wu_d = din("w_up", [4, D, DFF])
    wd_d = din("w_down", [4, DFF, D])
    out_d = nc.dram_tensor("out", [NSEQ * S_LEN, D], F32, kind="ExternalOutput").ap()

    def dscr(name, shape, dt=BF16):
        return nc.dram_tensor(name, list(shape), dt).ap()

    wqa_img = [dscr("wqa_img%d" % j, [128, 8 * QKVA]) for j in range(2)]
    wqb_img = [dscr("wqb_img%d" % j, [128, 8 * QKVB]) for j in range(2)]
    wg_img = [dscr("wg_img%d" % l, [NFC // 2, 128, 8 * 256]) for l in range(4)]
    wu_img = [dscr("wu_img%d" % l, [NFC // 2, 128, 8 * 256]) for l in range(4)]
    wo_img = [dscr("wo_img%d" % l, [128, 8 * D]) for l in range(4)]
    wd_img = [dscr("wd_img%d" % l, [128, NFC * D]) for l in range(4)]
    MODV = [nc.dram_tensor("modv%d" % l, [NSEQ, 6 * D], F32).ap() for l in range(4)]
    B_modv = [Buf("modv%d" % l) for l in range(4)]
    QT_d = dscr("QT_s", [D, S_LEN])
    KT_d = dscr("KT_s", [D, S_LEN])
    V_d = dscr("V_s", [S_LEN, D])
    AOT_d = dscr("AOT_s", [D, S_LEN])
    QKA_d = dscr("QKA_s", [S_LEN, 1280])
    VA_d = dscr("VA_s", [S_LEN, 256])
    B_QT, B_KT, B_V, B_AOT, B_QKA, B_VA = (Buf(n) for n in ("QT", "KT", "V", "AOT", "QKA", "VA"))
    B_img = {}
    xbuf = [[Buf("x%d_%d" % (b, i)) for i in range(NT)] for b in range(NSEQ)]

    inv_freq = np.power(np.float32(ROPE_THETA), -np.arange(8, dtype=np.float32) * np.float32(2.0) / np.float32(16)).astype(np.float32)

    with ExitStack() as st:
        S = Sched(nc, st)
        AR = Arena(nc, S, SB_BASE, SB_LIMIT)
        psum = st.enter_context(nc.psum_tensor("psum_all", [128, 4096], F32))

        def bank(i, n=1):
            return psum[:, i * 512:(i + n) * 512]

        def bank_bf(i):
            return psum[:, i * 512:(i + 1) * 512].bitcast(BF16)

        ident = AR.alloc("ident", [128, 128], BF16)
        identf = AR.alloc("identf", [128, 128], F32)
        onesb = AR.alloc("onesb", [128, 1024], BF16)
        onesf = AR.alloc("onesf", [128, 128], F32)
        Lmat = AR.alloc("Lmat", [128, 128], BF16)
        tribig = AR.alloc("tribig", [128, 128], BF16)
        negident = AR.alloc("negident", [128, 128], BF16)
        zerob = AR.alloc("zerob", [128, 128], BF16)
        MOD = AR.alloc("MOD", [128, 6, D], F32)
        wo_s = AR.alloc("wo_s", [128, 8, D], BF16)
        wd_s = AR.alloc("wd_s", [128, NFC, D], BF16)
        gain_rep = AR.alloc("gain_rep", [128, 18, HD], F32)
        esink = AR.alloc("esink", [128, 16], F32)
        cs_cos_l = [AR.alloc("cs_cos%d" % i, [128, NT, 8], F32) for i in range(NSEQ)]
        cs_sin_l = [AR.alloc("cs_sin%d" % i, [128, NT, 8], F32) for i in range(NSEQ)]
        invf = AR.alloc("invf", [128, 8], F32)
        PH = AR.mark()

        def mm(out, lhsT, rhs, start, stop, reads, writes, partial):
            S.op("pe", lambda e: e.matmul(out, lhsT=lhsT, rhs=rhs, start=start, stop=stop),
                 reads=reads, writes=writes, partial=partial)

        def tr(out, in_, idn, reads, writes, partial):
            S.op("pe", lambda e: e.transpose(out, in_, idn), reads=reads, writes=writes, partial=partial)

        def act(out, in_, func, reads, writes, partial=False, eng="act", **kw):
            S.op(eng, lambda e: e.activation(out=out, in_=in_, func=func, **kw), reads=reads, writes=writes, partial=partial)

        def tt(eng, out, in0, in1, op, reads, writes, partial=False):
            S.op(eng, lambda e: e.tensor_tensor(out=out, in0=in0, in1=in1, op=op), reads=reads, writes=writes, partial=partial)

        def ts(eng, out, in0, s1, s2, op0, op1, reads, writes, partial=False):
            if s2 is None:
                S.op(eng, lambda e: e.tensor_scalar(out=out, in0=in0, scalar1=s1, scalar2=None, op0=op0), reads=reads, writes=writes, partial=partial)
            else:
                S.op(eng, lambda e: e.tensor_scalar(out=out, in0=in0, scalar1=s1, scalar2=s2, op0=op0, op1=op1), reads=reads, writes=writes, partial=partial)

        def stt(out, in0, scalar, in1, op0, op1, reads, writes, partial=False):
            S.op("dve", lambda e: e.scalar_tensor_tensor(out=out, in0=in0, scalar=scalar, in1=in1, op0=op0, op1=op1),
                 reads=reads, writes=writes, partial=partial)

        def cp(eng, out, in_, reads, writes, partial=False):
            if eng == "act":
                act(out, in_, AF.Copy, reads, writes, partial)
            else:
                S.op(eng, lambda e: e.tensor_copy(out=out, in_=in_), reads=reads, writes=writes, partial=partial)

        def memset(eng, ap, val, writes, partial=False):
            S.op(eng, lambda e: e.memset(ap, val), writes=writes, partial=partial)

        memset("pool", onesb[:], 1.0, [onesb.b])
        memset("pool", onesf[:], 1.0, [onesf.b])
        S.op("pool", lambda e: e.affine_select(out=ident[:], in_=onesb[:, 0:128], pattern=[[-1, 128]], compare_op=ALU.is_equal,
                                               fill=0.0, base=0, channel_multiplier=1), reads=[onesb.b], writes=[ident.b])
        S.op("pool", lambda e: e.affine_select(out=identf[:], in_=onesf[:], pattern=[[-1, 128]], compare_op=ALU.is_equal,
                                               fill=0.0, base=0, channel_multiplier=1), reads=[onesf.b], writes=[identf.b])
        S.op("pool", lambda e: e.affine_select(out=Lmat[:], in_=onesb[:, 0:128], pattern=[[-1, 128]], compare_op=ALU.is_ge,
                                               fill=0.0, base=0, channel_multiplier=1), reads=[onesb.b], writes=[Lmat.b])
        memset("pool", zerob[:], 0.0, [zerob.b])
        ts("pool", negident[:], ident[:], -1.0, None, ALU.mult, None, [ident.b], [negident.b])
        S.op("pool", lambda e: e.affine_select(out=tribig[:], in_=zerob[:], pattern=[[1, 128]], compare_op=ALU.is_gt,
                                               fill=30000.0, base=0, channel_multiplier=-1),
             reads=[zerob.b], writes=[tribig.b])
        for j in range(8):
            memset("pool", invf[:, j:j + 1], float(inv_freq[j]), [invf.b], partial=True)

        def convert_all():
            mk = AR.mark()
            NSL = 4
            stf = [AR.alloc("cvf%d" % i, [128, 8, 256], F32) for i in range(NSL)]
            sth = [AR.alloc("cvh%d" % i, [128, 8, 256], BF16) for i in range(NSL)]
            jobs = []
            for j in range(2):
                if any(l % 2 == 0 and l // 2 == j for l in layers):
                    for c0 in range(0, QKVA, 256):
                        jobs.append((wqkva_d[j], c0, wqa_img[j].rearrange("p (kc n) -> p kc n", kc=8)[:, :, c0:c0 + 256], ("wqa", j)))
                if any(l % 2 == 1 and l // 2 == j for l in layers):
                    for c0 in range(0, QKVB, 256):
                        jobs.append((wqkvb_d[j], c0, wqb_img[j].rearrange("p (kc n) -> p kc n", kc=8)[:, :, c0:c0 + 256], ("wqb", j)))
            for l in layers:
                for g in range(NFC // 2):
                    jobs.append((wg_d[l], g * 256, wg_img[l][g].rearrange("p (kc n) -> p kc n", kc=8), ("wg", l)))
                    jobs.append((wu_d[l], g * 256, wu_img[l][g].rearrange("p (kc n) -> p kc n", kc=8), ("wu", l)))
            for l in layers:
                wo_src = (woa_d if l % 2 == 0 else wob_d)[l // 2]
                for kc in range(8):
                    jobs.append((wo_src[kc * 128:(kc + 1) * 128, :], None, wo_img[l].rearrange("p (kc n) -> p kc n", kc=8)[:, kc, :], ("wo", l)))
                for fc in range(NFC):
                    jobs.append((wd_d[l][fc * 128:(fc + 1) * 128, :], None, wd_img[l].rearrange("p (kc n) -> p kc n", kc=NFC)[:, fc, :], ("wd", l)))
            engs = ["dve", "pool"]
            NJ = len(jobs)

            def aps(n):
                w_ap, c0, dst, key = jobs[n]
                sl = n % NSL
                if c0 is None:
                    return (stf[sl][:].rearrange("p k n -> p (k n)")[:, 0:D], sth[sl][:].rearrange("p k n -> p (k n)")[:, 0:D], w_ap)
                return (stf[sl][:], sth[sl][:], w_ap.rearrange("(kc p) n -> p kc n", p=128)[:, :, c0:c0 + 256])

            def ld(n):
                f_ap, h_ap, src = aps(n)
                sl = n % NSL
                S.dma("sp", f_ap, src, stf[sl].b, writes=[stf[sl].b])

            AHEAD = 2
            for n in range(min(AHEAD, NJ)):
                ld(n)
            for n in range(NJ):
                w_ap, c0, dst, key = jobs[n]
                sl = n % NSL
                if key not in B_img:
                    B_img[key] = Buf("img_%s_%d" % key)
                if n + AHEAD < NJ:
                    ld(n + AHEAD)
                f_ap, h_ap, src = aps(n)
                cp(engs[n % 2], h_ap, f_ap, [stf[sl].b], [sth[sl].b])
                S.dma("act", dst, h_ap, sth[sl].b, reads=[sth[sl].b], writes=[B_img[key]], partial=True)
            S.barrier()
            AR.reset(mk)

        def rope_tables(b):
            mk = AR.mark()
            cs_cos, cs_sin = cs_cos_l[b], cs_sin_l[b]
            pi_t = AR.alloc("pi", [NT, 128], I32)
            pf_t = AR.alloc("pf", [NT, 128], F32)
            posf = AR.alloc("posf", [128, NT], F32)
            ang = AR.alloc("ang", [128, NT, 8], F32)
            kf = AR.alloc("kf", [128, NT, 8], F32)
            ki = AR.alloc("ki", [128, NT, 8], I32)
            rr = AR.alloc("rr", [128, NT, 8], F32)
            m1 = AR.alloc("m1", [128, NT, 8], F32)
            pp = T(bank(0)[:, 0:NT], "pp")
            S.dma("sp", pi_t[:], pos_d[b].rearrange("(n p) -> n p", p=128), pi_t.b, writes=[pi_t.b])
            cp("dve", pf_t[:], pi_t[:], [pi_t.b], [pf_t.b])
            tr(pp.h, pf_t[:], identf[0:NT, 0:NT], [pf_t.b, identf.b], [pp.b], False)
            cp("dve", posf[:], pp.h, [pp.b], [posf.b])
            tt("dve", ang[:], posf[:].unsqueeze(2).to_broadcast([128, NT, 8]), invf[:].unsqueeze(1).to_broadcast([128, NT, 8]), ALU.mult,
               [posf.b, invf.b], [ang.b])
            TWO_PI = 2.0 * np.pi
            C1 = 6.28125
            C2 = TWO_PI - C1

            def reduce_and_sin(dst, shift):
                ts("dve", kf[:], ang[:], 1.0 / TWO_PI, 0.5 + shift / TWO_PI, ALU.mult, ALU.add, [ang.b], [kf.b])
                cp("dve", ki[:], kf[:], [kf.b], [ki.b])
                cp("dve", kf[:], ki[:], [ki.b], [kf.b])
                stt(rr[:], kf[:], -C1, ang[:], ALU.mult, ALU.add, [kf.b, ang.b], [rr.b])
                stt(rr[:], kf[:], -C2, rr[:], ALU.mult, ALU.add, [kf.b, rr.b], [rr.b])
                if shift != 0.0:
                    ts("dve", rr[:], rr[:], shift, None, ALU.add, None, [rr.b], [rr.b])
                ts("dve", m1[:], rr[:], -np.pi, None, ALU.is_lt, None, [rr.b], [m1.b])
                stt(rr[:], m1[:], TWO_PI, rr[:], ALU.mult, ALU.add, [m1.b, rr.b], [rr.b])
                ts("dve", m1[:], rr[:], np.pi, None, ALU.is_gt, None, [rr.b], [m1.b])
                stt(rr[:], m1[:], -TWO_PI, rr[:], ALU.mult, ALU.add, [m1.b, rr.b], [rr.b])
                ts("dve", rr[:], rr[:], np.pi, -np.pi, ALU.min, ALU.max, [rr.b], [rr.b])
                act(dst[:], rr[:], AF.Sin, [rr.b], [dst.b])
            reduce_and_sin(cs_sin, 0.0)
            reduce_and_sin(cs_cos, np.pi / 2)
            S.barrier()
            AR.reset(mk)

        def mod_all():
            mk = AR.mark()
            crow = AR.alloc("crow", [1, NSEQ, D], F32)
            erow = AR.alloc("erow", [1, NSEQ, D], F32)
            condr = AR.alloc("condr", [1, NSEQ, D], F32)
            condT = AR.alloc("condT", [128, 8, NSEQ], F32)
            aw = [AR.alloc("aw%d" % i, [128, 8, 512], F32) for i in range(2)]
            bias = [AR.alloc("bias%d" % i, [NSEQ, 512], F32) for i in range(2)]
            gbc = [AR.alloc("gbc%d" % i, [NSEQ, 512], F32) for i in range(2)]
            tmpm = [AR.alloc("tmpm%d" % i, [NSEQ, 512], F32) for i in range(2)]
            mrow = [AR.alloc("mrow", [NSEQ, 6 * D], F32)] * 2
            pcT = T(bank(0)[:, 0:8 * NSEQ], "pcT")
            pm = [T(bank(2 + i)[0:NSEQ, :], "pm%d" % i) for i in range(2)]
            S.dma("sp", crow[:].rearrange("o b d -> o (b d)"), c_d.rearrange("b d -> (b d)").unsqueeze(0), crow.b, writes=[crow.b])
            act(erow[:], crow[:], AF.Exp, [crow.b], [erow.b], scale=-1.0)
            ts("dve", erow[:], erow[:], 1.0, None, ALU.add, None, [erow.b], [erow.b])
            S.op("dve", lambda e: e.reciprocal(out=erow[:], in_=erow[:]), reads=[erow.b], writes=[erow.b])
            tt("dve", condr[:], crow[:], erow[:], ALU.mult, [crow.b, erow.b], [condr.b])
            first = True
            for kc in range(8):
                for bb in range(NSEQ):
                    mm(pcT.h[:, kc * NSEQ + bb:kc * NSEQ + bb + 1], condr[0:1, bb, kc * 128:(kc + 1) * 128], onesf[0:1, 0:1], True, True,
                       [condr.b, onesf.b], [pcT.b], not first)
                    first = False
            cp("dve", condT[:].rearrange("p k m -> p (k m)"), pcT.h, [pcT.b], [condT.b])
            nj = 0
            for li, L in enumerate(layers):
                awv = ada_w_d[L].rearrange("(kc p) n -> p kc n", p=128)
                mr = mrow[li % 2]
                for j in range(12):
                    v, half = j // 2, j % 2
                    a = aw[nj % 2]
                    S.dma("sp", a[:], awv[:, :, j * 512:(j + 1) * 512], a.b, writes=[a.b])
                    bt = bias[nj % 2]
                    S.dma("sp", bt[:], ada_b_d[L:L + 1, j * 512:(j + 1) * 512].partition_broadcast(NSEQ), bt.b, writes=[bt.b])
                    p = pm[nj % 2]
                    for kc in range(8):
                        mm(p.h, condT[:, kc, :], a[:, kc, :], kc == 0, kc == 7, [condT.b, a.b], [p.b], kc > 0)
                    dst = mr[:, j * 512:(j + 1) * 512]
                    if v in (1, 4):
                        gt = gbc[half]
                        gsrc = n1g_d if v == 1 else n2g_d
                        S.dma("sp", gt[:], gsrc[L:L + 1, half * 512:(half + 1) * 512].partition_broadcast(NSEQ), gt.b, writes=[gt.b])
                        tm = tmpm[half]
                        tt("dve", tm[:], p.h, bt[:], ALU.add, [p.b, bt.b], [tm.b])
                        stt(dst, tm[:], 1.0, gt[:], ALU.add, ALU.mult, [tm.b, gt.b], [mr.b], j > 0)
                    else:
                        tt("dve", dst, p.h, bt[:], ALU.add, [p.b, bt.b], [mr.b], j > 0)
                    nj += 1
                S.dma("sp", MODV[L], mr[:], mr.b, reads=[mr.b], writes=[B_modv[L]])
            S.barrier()
            AR.reset(mk)

        def layer_setup(L):
            isA = (L % 2 == 0)
            Lm = L // 2
            S.dma("sp", wo_s[:].rearrange("p k n -> p (k n)"), wo_img[L], wo_s.b, reads=[B_img[("wo", L)]], writes=[wo_s.b])
            S.dma("sp", wd_s[:].rearrange("p k n -> p (k n)"), wd_img[L], wd_s.b, reads=[B_img[("wd", L)]], writes=[wd_s.b])
            if isA:
                mk = AR.mark()
                gq = AR.alloc("gq", [128, 2, HD], F32)
                sk = AR.alloc("sk", [128, 16], F32)
                S.dma("sp", gq[:, 0, :], qna_d[Lm:Lm + 1, :].partition_broadcast(128), gq.b, writes=[gq.b], partial=True)
                S.dma("sp", gq[:, 1, :], kna_d[Lm:Lm + 1, :].partition_broadcast(128), gq.b, writes=[gq.b], partial=True, chain=True)
                S.dma("sp", sk[:], sinks_d[Lm:Lm + 1, :].partition_broadcast(128), sk.b, writes=[sk.b])
                cp("dve", gain_rep[:, 0:16, :], gq[:, 0:1, :].to_broadcast([128, 16, HD]), [gq.b], [gain_rep.b], True)
                cp("dve", gain_rep[:, 16:18, :], gq[:, 1:2, :].to_broadcast([128, 2, HD]), [gq.b], [gain_rep.b], True)
                act(esink[:], sk[:], AF.Exp, [sk.b], [esink.b])
                S.barrier()
                AR.reset(mk)

        def mod_load(L, b):
            S.dma("sp", MOD[:].rearrange("p v d -> p (v d)"), MODV[L][b:b + 1, :].partition_broadcast(128), MOD.b,
                  reads=[B_modv[L]], writes=[MOD.b])

        def norm_mod_T(xt, slotA, slotB, tmpf, hb, junk, ssq, pT, hT, tcol, first):
            act(hb[:], xt[:], AF.Square, [xt.b], [hb.b, ssq.b], accum_out=ssq[:])
            act(ssq[:], ssq[:], AF.Ln, [ssq.b], [ssq.b], scale=1.0 / D, bias=EPS_T[:])
            act(ssq[:], ssq[:], AF.Exp, [ssq.b], [ssq.b], scale=-0.5)
            stt(tmpf[:], xt[:], ssq[:, 0:1], MOD[:, slotA, :], ALU.mult, ALU.mult, [xt.b, ssq.b, MOD.b], [tmpf.b])
            tt("pool", hb[:], tmpf[:], MOD[:, slotB, :], ALU.add, [tmpf.b, MOD.b], [hb.b])
            pv = pT.h.rearrange("p (k t) -> p k t", t=128)
            for kc in range(8):
                tr(pv[:, kc, :], hb[:, kc * 128:(kc + 1) * 128], ident[:], [hb.b, ident.b], [pT.b], kc > 0)
            cp("act", hT[:, :, tcol * 128:(tcol + 1) * 128], pv, [pT.b], [hT.b], not first)

        def p1_phase(L, b, first_layer):
            mk = AR.mark()
            isA = (L % 2 == 0)
            Lm = L // 2
            NW = QKVA if isA else QKVB
            Wq = AR.alloc("Wq", [128, 8, NW], BF16)
            img = (wqa_img if isA else wqb_img)[Lm]
            ikey = ("wqa" if isA else "wqb", Lm)
            S.dma("sp", Wq[:].rearrange("p k n -> p (k n)"), img, Wq.b, reads=[B_img[ikey]], writes=[Wq.b])
            xts = [AR.alloc("xt%d" % i, [128, D], F32) for i in range(3)]
            tmpf = [AR.alloc("tmpf%d" % i, [128, D], F32) for i in range(2)]
            hb = [AR.alloc("hb%d" % i, [128, D], BF16) for i in range(2)]
            junk = None
            ssq = [AR.alloc("ssq%d" % i, [128, 1], F32) for i in range(2)]
            hT = [AR.alloc("hT%d" % i, [128, 8, 512], BF16) for i in range(2)]
            pT = [T(bank_bf(i), "pT%d" % i) for i in range(2)]
            xsrc = x_d if first_layer else out_d
            if isA:
                pq = [T(bank(2 + 3 * i, 3), "pq%d" % i) for i in range(2)]
                sqf = [AR.alloc("sqf%d" % i, [128, 18, HD], F32) for i in range(2)]
                qn = [AR.alloc("qn%d" % i, [128, 18, HD], F32) for i in range(2)]
                qg = [AR.alloc("qg%d" % i, [128, 18, HD], F32) for i in range(3)]
                s18 = [AR.alloc("s18_%d" % i, [128, 18], F32) for i in range(3)]
                rt = [[AR.alloc("rt%d_%d" % (j, i), [128, 18, 8], F32) for i in range(4)] for j in range(2)]
                qkb = [AR.alloc("qkb%d" % i, [128, 1280], BF16) for i in range(3)]
                vaug = [AR.alloc("vaug%d" % i, [128, 2, 128], BF16) for i in range(3)]
                for v in vaug:
                    memset("pool", v[:], 1.0, [v.b])
            else:
                pb = [T(bank(2 + i), "pb%d" % i) for i in range(6)]
                pv2 = [T(bank(2 + 2 * i, 2), "pv%d" % i) for i in range(3)]
                qst = [AR.alloc("qst%d" % i, [128, 4, 512], BF16) for i in range(2)]
                vst = [AR.alloc("vst%d" % i, [128, D], BF16) for i in range(2)]
            st = dict(nb=0)

            def normT(c, t4):
                i = c * 4 + t4
                row0 = b * S_LEN + i * 128
                xt = xts[i % 3]
                S.dma("sp", xt[:], xsrc[row0:row0 + 128, :], xt.b, reads=([] if first_layer else [xbuf[b][i]]), writes=[xt.b])
                norm_mod_T(xt, 1, 0, tmpf[i % 2], hb[i % 2], junk, ssq[i % 2], pT[i % 2], hT[c % 2], t4, t4 == 0)

            def projA(c, t4):
                i = c * 4 + t4
                hTc = hT[c % 2]
                p = pq[i % 2]
                for (c0, c1) in ((0, 512), (512, 1024), (1024, 1280)):
                    for kc in range(8):
                        mm(p.h[:, c0:c1], hTc[:, kc, t4 * 128:(t4 + 1) * 128], Wq[:, kc, c0:c1], kc == 0, kc == 7,
                           [hTc.b, Wq.b], [p.b], not (c0 == 0 and kc == 0))
                p3 = p.h[:, 0:1152].rearrange("p (h d) -> p h d", d=HD)
                sq = sqf[i % 2]
                act(sq[:].rearrange("p h d -> p (h d)"), p.h[:, 0:1152], AF.Square, [p.b], [sq.b])
                va = vaug[i % 3]
                cp("dve", va[:, :, 0:64], p.h[:, 1152:1280].rearrange("p (g d) -> p g d", d=64), [p.b], [va.b], True)
                s1 = s18[i % 3]
                S.op("dve", lambda e, s1=s1, sq=sq: e.reduce_sum(out=s1[:], in_=sq[:], axis=AX.X), reads=[sq.b], writes=[s1.b])
                act(s1[:], s1[:], AF.Ln, [s1.b], [s1.b], scale=1.0 / HD, bias=EPS_T[:])
                act(s1[:], s1[:], AF.Exp, [s1.b], [s1.b], scale=-0.5)
                q_n = qn[i % 2]
                tt("dve", q_n[:], p3, s1[:].unsqueeze(2).to_broadcast([128, 18, HD]), ALU.mult, [p.b, s1.b], [q_n.b])
                g = qg[i % 3]
                tt("pool", g[:], q_n[:], gain_rep[:], ALU.mult, [q_n.b, gain_rep.b], [g.b])
                cs_cos, cs_sin = cs_cos_l[b], cs_sin_l[b]
                cosb = cs_cos[:, i:i + 1, :].to_broadcast([128, 18, 8])
                sinb = cs_sin[:, i:i + 1, :].to_broadcast([128, 18, 8])
                r = rt[i % 2]
                tt("dve", r[0][:], g[:, :, 0:8], cosb, ALU.mult, [g.b, cs_cos.b], [r[0].b])
                tt("dve", r[1][:], g[:, :, 8:16], sinb, ALU.mult, [g.b, cs_sin.b], [r[1].b])
                tt("dve", r[2][:], g[:, :, 8:16], cosb, ALU.mult, [g.b, cs_cos.b], [r[2].b])
                tt("dve", r[3][:], g[:, :, 0:8], sinb, ALU.mult, [g.b, cs_sin.b], [r[3].b])
                tt("dve", g[:, :, 0:8], r[0][:], r[1][:], ALU.subtract, [r[0].b, r[1].b], [g.b], True)
                tt("dve", g[:, :, 8:16], r[2][:], r[3][:], ALU.add, [r[2].b, r[3].b], [g.b], True)
                qb = qkb[i % 3]
                cp("act", qb[:, 0:1024], g[:, 0:16, :].rearrange("p h d -> p (h d)"), [g.b], [qb.b])
                kv4 = qb[:, 1024:1280].rearrange("p (g u d) -> p g u d", g=2, u=2)
                for u in range(2):
                    cp("pool", kv4[:, :, u, :], g[:, 16:18, :], [g.b], [qb.b], True)
                S.dma("act", QKA_d[i * 128:(i + 1) * 128, :], qb[:], qb.b, reads=[qb.b], writes=[B_QKA], partial=True)
                S.dma("act", VA_d[i * 128:(i + 1) * 128, :], va[:].rearrange("p g d -> p (g d)"), va.b, reads=[va.b], writes=[B_VA], partial=True)

            def projB_qk(c, m):
                hTc = hT[c % 2]
                p = pb[st["nb"] % 6]
                st["nb"] += 1
                for kc in range(8):
                    mm(p.h, Wq[:, kc, m * 128:(m + 1) * 128], hTc[:, kc, :], kc == 0, kc == 7, [Wq.b, hTc.b], [p.b], kc > 0)
                qs = qst[(m // 4) % 2]
                if m >= 8:
                    cp("act" if m % 2 else "dve", qs[:, m % 4, :], p.h, [p.b], [qs.b], (m % 4) != 0)
                elif m % 2:
                    act(qs[:, m % 4, :], p.h, AF.Copy, [p.b], [qs.b], (m % 4) != 0, scale=0.125)
                else:
                    ts("dve", qs[:, m % 4, :], p.h, 0.125, None, ALU.mult, None, [p.b], [qs.b], (m % 4) != 0)
                if m % 4 == 3:
                    m0 = m - 3
                    dst_t, dst_b = (QT_d, B_QT) if m0 < 8 else (KT_d, B_KT)
                    r0 = (m0 % 8) * 128
                    S.dma("act", dst_t[r0:r0 + 512, c * 512:(c + 1) * 512].rearrange("(m p) n -> p m n", p=128), qs[:], qs.b,
                          reads=[qs.b], writes=[dst_b], partial=True)

            def projB_v(c, t4):
                i = c * 4 + t4
                hTc = hT[c % 2]
                p = pv2[i % 3]
                pbs = [pb[2 * (i % 3)].b, pb[2 * (i % 3) + 1].b]
                for nh in range(2):
                    for kc in range(8):
                        mm(p.h[:, nh * 512:(nh + 1) * 512], hTc[:, kc, t4 * 128:(t4 + 1) * 128], Wq[:, kc, 2048 + nh * 512:2048 + (nh + 1) * 512],
                           kc == 0, kc == 7, [hTc.b, Wq.b], [pbs[nh]], kc > 0)
                vs = vst[i % 2]
                cp("dve" if i % 2 else "act", vs[:], p.h, pbs, [vs.b])
                S.dma("act", V_d[i * 128:(i + 1) * 128, :], vs[:], vs.b, reads=[vs.b], writes=[B_V], partial=True)

            for t4 in range(4):
                normT(0, t4)
            for c in range(NCH):
                nxt = c + 1 < NCH
                for t4 in range(4):
                    if isA:
                        projA(c, t4)
                    else:
                        for m in range(4 * t4, 4 * t4 + 4):
                            projB_qk(c, m)
                        projB_v(c, t4)
                    if nxt:
                        normT(c + 1, t4)
            S.barrier()
            AR.reset(mk)

        def p2a_phase(L, b):
            mk = AR.mark()
            qin = [AR.alloc("qin%d" % i, [128, 1280], BF16) for i in range(2)]
            vaug = [AR.alloc("vaug%d" % i, [128, 2, 128], BF16) for i in range(3)]
            qT = [AR.alloc("qT%d" % i, [128, 8, 128], BF16) for i in range(2)]
            kT = [AR.alloc("kT%d" % i, [128, 2, 128], BF16) for i in range(3)]
            Pm = [AR.alloc("Pm%d" % i, [128, 1024], BF16) for i in range(3)]
            AO = [AR.alloc("AO%d" % i, [128, D], BF16) for i in range(2)]
            den = [AR.alloc("den%d" % i, [128, 4, 1], F32) for i in range(2)]
            aoT = [AR.alloc("aoT%d" % i, [128, 8, 512], BF16) for i in range(2)]
            amask = AR.alloc("amask", [128, 1024], BF16)
            am4 = amask[:].rearrange("p (a k t) -> p a k t", k=2, t=128)
            on3 = onesb[:, 0:512].rearrange("p (a t) -> p a t", t=128)
            S.op("pool", lambda e: e.affine_select(out=am4[:, :, 1, :], in_=on3, pattern=[[0, 4], [1, 128]], compare_op=ALU.is_ge,
                                                   fill=0.0, base=0, channel_multiplier=-1), reads=[onesb.b], writes=[amask.b], partial=True)
            S.op("pool", lambda e: e.affine_select(out=am4[:, :, 0, :], in_=on3, pattern=[[0, 4], [-1, 128]], compare_op=ALU.is_gt,
                                                   fill=0.0, base=0, channel_multiplier=1), reads=[onesb.b], writes=[amask.b], partial=True)
            pTq = T(bank_bf(0), "pTq")
            pTk = T(bank_bf(1), "pTk")
            pS = [T(bank(2 + 2 * i, 2), "pS%d" % i) for i in range(2)]
            pO = [T(bank(6 + i), "pO%d" % i) for i in range(2)]
            memset("pool", kT[2][:], 0.0, [kT[2].b])
            memset("pool", vaug[2][:], 0.0, [vaug[2].b])
            SCALE = HD ** -0.5
            pq3 = pTq.h.rearrange("p (k t) -> p k t", t=128)
            pk3 = pTk.h.rearrange("p (k t) -> p k t", t=128)

            def stageA(i):
                qi = qin[i % 2]
                va = vaug[i % 3]
                S.dma("sp", qi[:], QKA_d[i * 128:(i + 1) * 128, :], qi.b, reads=[B_QKA], writes=[qi.b])
                S.dma("sp", va[:].rearrange("p g d -> p (g d)"), VA_d[i * 128:(i + 1) * 128, :], va.b, reads=[B_VA], writes=[va.b])
                for kc in range(8):
                    tr(pq3[:, kc, :], qi[:, kc * 128:(kc + 1) * 128], ident[:], [qi.b, ident.b], [pTq.b], kc > 0)
                for g in range(2):
                    tr(pk3[:, g, :], qi[:, 1024 + g * 128:1024 + (g + 1) * 128], ident[:], [qi.b, ident.b], [pTk.b], g > 0)
                cp("dve", qT[i % 2][:], pq3, [pTq.b], [qT[i % 2].b])
                cp("act", kT[i % 3][:], pk3[:, 0:2, :], [pTk.b], [kT[i % 3].b])

            def sem(n):
                i, k = divmod(n, 4)
                g, par = divmod(k, 2)
                ps = pS[n % 2]
                first = True
                for cq in range(4):
                    for kt in range(2):
                        col = cq * 256 + kt * 128
                        kbuf = kT[(i - 1) % 3] if kt == 0 else kT[i % 3]
                        mm(ps.h[:, col:col + 128], kbuf[par * 64:(par + 1) * 64, g, :], qT[i % 2][par * 64:(par + 1) * 64, 4 * g + cq, :],
                           True, True, [kbuf.b, qT[i % 2].b], [ps.b], not first)
                        first = False
                P = Pm[n % 3]
                act(P[:], ps.h, AF.Exp, [ps.b], [P.b], scale=SCALE)
                tt("dve", P[:], P[:], amask[:], ALU.mult, [P.b, amask.b], [P.b])

            def pvn(n):
                i, k = divmod(n, 4)
                g, par = divmod(k, 2)
                P = Pm[n % 3]
                po = pO[n % 2]
                po3 = po.h.rearrange("p (h d) -> p h d", d=128)
                vp, vc = vaug[(i - 1) % 3], vaug[i % 3]
                for cq in range(4):
                    col = cq * 256
                    mm(po3[:, cq, :], P[:, col:col + 128], vp[:, g, :], True, False, [P.b, vp.b], [po.b], cq > 0)
                    mm(po3[:, cq, :], P[:, col + 128:col + 256], vc[:, g, :], False, True, [P.b, vc.b], [po.b], True)
                dn = den[n % 2]
                es = esink[:, 8 * g:8 * g + 8].rearrange("p (cq par) -> p cq par", par=2)[:, :, par:par + 1]
                tt("dve", dn[:], po3[:, :, 64:65], es, ALU.add, [po.b, esink.b], [dn.b])
                S.op("dve", lambda e, dn=dn: e.reciprocal(out=dn[:], in_=dn[:]), reads=[dn.b], writes=[dn.b])
                ao = AO[i % 2]
                aov = ao[:, g * 512:(g + 1) * 512].rearrange("p (cq par d) -> p cq par d", par=2, d=64)[:, :, par, :]
                tt("dve", aov, po3[:, :, 0:64], dn[:].to_broadcast([128, 4, 64]), ALU.mult, [po.b, dn.b], [ao.b], k > 0)

            def stageD(i):
                c, t4 = i // 4, i % 4
                ao = AO[i % 2]
                for kc in range(8):
                    tr(pq3[:, kc, :], ao[:, kc * 128:(kc + 1) * 128], ident[:], [ao.b, ident.b], [pTq.b], kc > 0)
                at = aoT[c % 2]
                cp("act", at[:, :, t4 * 128:(t4 + 1) * 128], pq3, [pTq.b], [at.b], t4 > 0)
                if t4 == 3:
                    S.dma("sp", AOT_d[:, c * 512:(c + 1) * 512].rearrange("(k p) n -> p k n", p=128), at[:], at.b, reads=[at.b], writes=[B_AOT], partial=True)

            NU = 4 * NT
            stageA(0)
            sem(0)
            for n in range(NU):
                i, k = divmod(n, 4)
                if k == 1 and i + 1 < NT:
                    stageA(i + 1)
                if n + 1 < NU:
                    sem(n + 1)
                pvn(n)
                if k == 3:
                    stageD(i)
            S.barrier()
            AR.reset(mk)

        def p2b_phase(L, b):
            mk = AR.mark()
            Q0 = [AR.alloc("Q0_%d" % i, [128, S_LEN], BF16) for i in range(2)]
            Q1 = [AR.alloc("Q1_%d" % i, [128, S_LEN], BF16) for i in range(2)]
            KTt = [AR.alloc("KTt%d" % i, [128, S_LEN], BF16) for i in range(2)]
            KTn = [AR.alloc("KTn%d" % i, [128, S_LEN], BF16) for i in range(2)]
            Vt = [AR.alloc("Vt%d" % i, [128, NT, 128], BF16) for i in range(2)]
            Ef = [AR.alloc("Ef%d" % i, [128, 1024], F32) for i in range(2)]
            SP = [AR.alloc("SP%d" % i, [128, 1024], BF16) for i in range(2)]
            Ab = [AR.alloc("Ab%d" % i, [128, 1024], BF16) for i in range(2)]
            Srun = [AR.alloc("Srun%d" % i, [128, 1024], BF16) for i in range(2)]
            ost = [AR.alloc("ost%d" % i, [128, 512], BF16) for i in range(2)]
            pZ = T(bank(0, 2), "pZ")
            pD = [T(bank(2 + 2 * i, 2), "pD%d" % i) for i in range(2)]
            pOT = T(bank(6, 2), "pOT")
            for i in range(2):
                memset("pool", Q0[i][64:128, :], 0.0, [Q0[i].b])
                memset("pool", Q1[i][0:64, :], 0.0, [Q1[i].b])
            its = []
            ng = 0
            for hp in range(8):
                for qc in range(NCH):
                    kbs = list(range(4 * qc + 3, -1, -1))
                    for n, kb in enumerate(kbs):
                        its.append(dict(hp=hp, qc=qc, kb=kb, first=(n == 0), last=(kb == 0), grp=ng, idx=len(its), k=n))
                    ng += 1
            loaded = set()

            def load_pair(hp):
                if hp in loaded or hp >= 8:
                    return
                loaded.add(hp)
                s = hp % 2
                S.dma("sp", Q0[s][0:64, :], QT_d[hp * 128:hp * 128 + 64, :], Q0[s].b, reads=[B_QT], writes=[Q0[s].b], partial=True)
                S.dma("sp", Q1[s][64:128, :], QT_d[hp * 128 + 64:hp * 128 + 128, :], Q1[s].b, reads=[B_QT], writes=[Q1[s].b], partial=True)
                S.dma("sp", KTt[s][:], KT_d[hp * 128:(hp + 1) * 128, :], KTt[s].b, reads=[B_KT], writes=[KTt[s].b])
                S.dma("sp", Vt[s][:], V_d[:, hp * 128:(hp + 1) * 128].rearrange("(k p) d -> p k d", p=128), Vt[s].b, reads=[B_V], writes=[Vt[s].b])
                ts("dve", KTn[s][:], KTt[s][:], -1.0, None, ALU.mult, None, [KTt[s].b], [KTn[s].b])

            def cols(it, par):
                r = it["kb"] - 4 * it["qc"]
                c0 = 128 * r if r >= 0 else 0
                return c0, slice(par * 512 + c0, (par + 1) * 512)

            def v3(ap, it):
                r = it["kb"] - 4 * it["qc"]
                if r <= 0:
                    return ap
                return ap.rearrange("p (a f) -> p a f", a=2)[:, :, 128 * r:512]

            def zmm(it):
                hp, qc, kb, n = it["hp"], it["qc"], it["kb"], it["idx"]
                load_pair(hp)
                s = hp % 2
                r = kb - 4 * qc
                Qp = (Q0[s], Q1[s])
                for par in range(2):
                    c0, cs = cols(it, par)
                    mm(pZ.h[:, cs], KTt[s][:, kb * 128:(kb + 1) * 128], Qp[par][:, qc * 512 + c0:(qc + 1) * 512], True, r < 0,
                       [KTt[s].b, Qp[par].b], [pZ.b], par > 0)
                    if r >= 0:
                        mm(pZ.h[:, par * 512 + c0:par * 512 + c0 + 128], negident[:], tribig[:], False, True,
                           [negident.b, tribig.b], [pZ.b], True)

            def es(it):
                n = it["idx"]
                ef = Ef[n % 2]
                act(v3(ef[:], it), v3(pZ.h, it), AF.Exp, [pZ.b], [ef.b])
                sp = SP[n % 2]
                act(v3(sp[:], it), v3(ef[:], it), AF.Ln, [ef.b], [sp.b], bias=ONE_T[:])

            def dmm(it):
                hp, qc, kb, n, k = it["hp"], it["qc"], it["kb"], it["idx"], it["k"]
                s = hp % 2
                r = kb - 4 * qc
                Qp = (Q0[s], Q1[s])
                sp = SP[n % 2]
                d = pD[n % 2]
                sr_prev = Srun[(k - 1) % 2]
                for par in range(2):
                    c0, cs = cols(it, par)
                    mm(d.h[:, cs], Lmat[:], sp[:, cs], True, False, [Lmat.b, sp.b], [d.b], par > 0)
                    if not it["first"]:
                        mm(d.h[:, cs], onesb[:, 0:128], sr_prev[:, cs], False, False, [onesb.b, sr_prev.b], [d.b], True)
                    mm(d.h[:, cs], KTn[s][:, kb * 128:(kb + 1) * 128], Qp[par][:, qc * 512 + c0:(qc + 1) * 512], False, r < 0,
                       [KTn[s].b, Qp[par].b], [d.b], True)
                    if r >= 0:
                        mm(d.h[:, par * 512 + c0:par * 512 + c0 + 128], ident[:], tribig[:], False, True, [ident.b, tribig.b], [d.b], True)

            def aexp(it):
                n = it["idx"]
                act(v3(Ab[n % 2][:], it), v3(pD[n % 2].h, it), AF.Exp, [pD[n % 2].b], [Ab[n % 2].b], scale=-1.0)

            def sadd(it):
                n, k = it["idx"], it["k"]
                sp = SP[n % 2]
                sr_prev = Srun[(k - 1) % 2]
                sr_new = Srun[k % 2]
                if it["first"]:
                    memset("pool", sr_prev[:], 0.0, [sr_prev.b])
                    memset("pool", sr_new[:], 0.0, [sr_new.b])
                if not it["last"]:
                    if it["first"]:
                        cp("dve", v3(sr_new[:], it), v3(sp[:], it), [sp.b], [sr_new.b], True)
                    else:
                        tt("dve", v3(sr_new[:], it), v3(sr_prev[:], it), v3(sp[:], it), ALU.add, [sr_prev.b, sp.b], [sr_new.b])

            def pv(it):
                hp, qc, kb, n = it["hp"], it["qc"], it["kb"], it["idx"]
                s = hp % 2
                ab = Ab[n % 2]
                if it["first"]:
                    for par in range(2):
                        mm(pOT.h[:, par * 512:(par + 1) * 512], zerob[:], Q0[s][:, qc * 512:(qc + 1) * 512], True, False,
                           [zerob.b, Q0[s].b], [pOT.b], par > 0)
                for par in range(2):
                    c0, cs = cols(it, par)
                    mm(pOT.h[:, cs], Vt[s][:, kb, :], ab[:, cs], False, it["last"], [Vt[s].b, ab.b], [pOT.b], True)
                if it["last"]:
                    o = ost[it["grp"] % 2]
                    cp("dve", o[0:64, :], pOT.h[0:64, 0:512], [pOT.b], [o.b])
                    cp("dve", o[64:128, :], pOT.h[64:128, 512:1024], [pOT.b], [o.b], True)
                    S.dma("sp", AOT_d[hp * 128:(hp + 1) * 128, qc * 512:(qc + 1) * 512], o[:], o.b, reads=[o.b], writes=[B_AOT], partial=True)
                    if qc == 0:
                        load_pair(hp + 1)

            NI = len(its)
            load_pair(0)
            zmm(its[0])
            es(its[0])
            if NI > 1:
                zmm(its[1])
            for n in range(NI):
                if n + 1 < NI:
                    es(its[n + 1])
                dmm(its[n])
                if n >= 1:
                    pv(its[n - 1])
                if n + 2 < NI:
                    zmm(its[n + 2])
                aexp(its[n])
                sadd(its[n])
            pv(its[NI - 1])
            S.barrier()
            AR.reset(mk)

        def p3_phase(L, b, first_layer):
            mk = AR.mark()
            xts = [AR.alloc("xt%d" % i, [128, D], F32) for i in range(5)]
            tmpf = AR.alloc("tmpf", [128, D], F32)
            hb = [AR.alloc("hb%d" % i, [128, D], BF16) for i in range(2)]
            junk = None
            tmpg = AR.alloc("tmpg", [128, D], F32)
            ssq = [AR.alloc("ssq%d" % i, [128, 1], F32) for i in range(2)]
            hT = AR.alloc("hT", [128, 8, 512], BF16)
            aoT = [AR.alloc("aoT%d" % i, [128, 8, 512], BF16) for i in range(2)]
            actT = AR.alloc("actT", [128, NFC, 512], BF16)
            wgu = [AR.alloc("wgu%d" % i, [128, 2, 8 * 256], BF16) for i in range(2)]
            ef = [AR.alloc("ef%d" % i, [128, 512], F32) for i in range(2)]
            tf = [AR.alloc("tf%d" % i, [128, 512], F32) for i in range(2)]
            pY = T(bank(0, 2), "pY")
            pT = [T(bank_bf(2), "pT0"), T(bank_bf(7), "pT1")]
            pG = [T(bank(3 + 2 * i), "pG%d" % i) for i in range(2)]
            pU = [T(bank(4 + 2 * i), "pU%d" % i) for i in range(2)]
            pW = T(bank(3, 2), "pW")
            pW_bufs = [pG[0].b, pU[0].b]
            xsrc = x_d if first_layer else out_d
            st = dict(nx=0, nf=0)
            xl = {}

            def load_ao(c):
                at = aoT[c % 2]
                S.dma("sp", at[:], AOT_d[:, c * 512:(c + 1) * 512].rearrange("(k p) n -> p k n", p=128), at.b, reads=[B_AOT], writes=[at.b])

            def load_x(c, t4):
                i = c * 4 + t4
                row0 = b * S_LEN + i * 128
                xt = xts[st["nx"] % 5]
                st["nx"] += 1
                xl[(c, t4)] = xt
                S.dma("sp", xt[:], xsrc[row0:row0 + 128, :], xt.b, reads=([] if first_layer else [xbuf[b][i]]), writes=[xt.b])

            def front_wo(c, t4):
                i = c * 4 + t4
                at = aoT[c % 2]
                xt = xl[(c, t4)]
                for nh in range(2):
                    for kc in range(8):
                        mm(pW.h[:, nh * 512:(nh + 1) * 512], at[:, kc, t4 * 128:(t4 + 1) * 128], wo_s[:, kc, nh * 512:(nh + 1) * 512],
                           kc == 0, kc == 7, [at.b, wo_s.b], [pW_bufs[nh]], kc > 0)
                tt("dve", tmpg[:], pW.h, MOD[:, 2, :], ALU.mult, pW_bufs + [MOD.b], [tmpg.b])
                tt("pool", xt[:], tmpg[:], xt[:], ALU.add, [tmpg.b, xt.b], [xt.b])
                hbt = hb[i % 2]
                sq = ssq[i % 2]
                act(hbt[:], xt[:], AF.Square, [xt.b], [hbt.b, sq.b], accum_out=sq[:])
                act(sq[:], sq[:], AF.Ln, [sq.b], [sq.b], scale=1.0 / D, bias=EPS_T[:])
                act(sq[:], sq[:], AF.Exp, [sq.b], [sq.b], scale=-0.5)
                stt(tmpf[:], xt[:], sq[:, 0:1], MOD[:, 4, :], ALU.mult, ALU.mult, [xt.b, sq.b, MOD.b], [tmpf.b])
                tt("pool", hbt[:], tmpf[:], MOD[:, 3, :], ALU.add, [tmpf.b, MOD.b], [hbt.b])

            def front_T(c, t4):
                i = c * 4 + t4
                hbt = hb[i % 2]
                p = pT[i % 2]
                pv = p.h.rearrange("p (k t) -> p k t", t=128)
                for kc in range(8):
                    tr(pv[:, kc, :], hbt[:, kc * 128:(kc + 1) * 128], ident[:], [hbt.b, ident.b], [p.b], kc > 0)
                cp("act", hT[:, :, t4 * 128:(t4 + 1) * 128], pv, [p.b], [hT.b], t4 > 0)

            def gateup(c):
                for fg in range(NFC // 2):
                    w = wgu[fg % 2]
                    S.dma("sp", w[:, 0, :], wg_img[L][fg], w.b, reads=[B_img[("wg", L)]], writes=[w.b], partial=False)
                    S.dma("sp", w[:, 1, :], wu_img[L][fg], w.b, reads=[B_img[("wu", L)]], writes=[w.b], partial=True, chain=True)
                    if fg == 1 and c + 1 < NCH:
                        load_ao(c + 1)
                        load_x(c + 1, 0)
                    wg3 = w[:, 0, :].rearrange("p (k n) -> p k n", k=8)
                    wu3 = w[:, 1, :].rearrange("p (k n) -> p k n", k=8)
                    for j in range(2):
                        f = fg * 2 + j
                        nf = st["nf"]
                        g_, u_ = pG[nf % 2], pU[nf % 2]
                        e_, t_ = ef[nf % 2], tf[nf % 2]
                        st["nf"] += 1
                        for kc in range(8):
                            mm(g_.h, wg3[:, kc, j * 128:(j + 1) * 128], hT[:, kc, :], kc == 0, kc == 7, [w.b, hT.b], [g_.b], kc > 0)
                        for kc in range(8):
                            mm(u_.h, wu3[:, kc, j * 128:(j + 1) * 128], hT[:, kc, :], kc == 0, kc == 7, [w.b, hT.b], [u_.b], kc > 0)
                        act(e_[:], g_.h, AF.Exp, [g_.b], [e_.b], scale=-1.0)
                        act(e_[:], e_[:], AF.Ln, [e_.b], [e_.b], bias=ONE_T[:])
                        act(e_[:], e_[:], AF.Exp, [e_.b], [e_.b], scale=-1.0)
                        tt("dve", t_[:], g_.h, e_[:], ALU.mult, [g_.b, e_.b], [t_.b])
                        tt("dve", actT[:, f, :], u_.h, t_[:], ALU.mult, [u_.b, t_.b], [actT.b], f > 0)

            def down(c, t4):
                i = c * 4 + t4
                row0 = b * S_LEN + i * 128
                xt = xl[(c, t4)]
                for nh in range(2):
                    for fc in range(NFC):
                        mm(pY.h[:, nh * 512:(nh + 1) * 512], actT[:, fc, t4 * 128:(t4 + 1) * 128], wd_s[:, fc, nh * 512:(nh + 1) * 512],
                           fc == 0, fc == NFC - 1, [actT.b, wd_s.b], [pY.b], not (nh == 0 and fc == 0))
                tt("dve", tmpg[:], pY.h, MOD[:, 5, :], ALU.mult, [pY.b, MOD.b], [tmpg.b])
                tt("pool", xt[:], tmpg[:], xt[:], ALU.add, [tmpg.b, xt.b], [xt.b])
                S.dma("act", out_d[row0:row0 + 128, :], xt[:], xt.b, reads=[xt.b], writes=[xbuf[b][i]])

            load_ao(0)
            for t4 in range(4):
                load_x(0, t4)
            for t4 in range(4):
                front_wo(0, t4)
                front_T(0, t4)
            for c in range(NCH):
                gateup(c)
                nxt = c + 1 < NCH
                for t4 in range(4):
                    down(c, t4)
                    if nxt:
                        if t4 >= 1:
                            front_T(c + 1, t4 - 1)
                        if t4 >= 1:
                            load_x(c + 1, t4)
                        front_wo(c + 1, t4)
                if nxt:
                    front_T(c + 1, 3)
            S.barrier()
            AR.reset(mk)

        EPS_T = AR.alloc("eps_t", [128, 1], F32)
        ONE_T = AR.alloc("one_t", [128, 1], F32)
        memset("pool", EPS_T[:], EPS, [EPS_T.b])
        memset("pool", ONE_T[:], 1.0, [ONE_T.b])
        PH = AR.mark()
        if do_prologue:
            convert_all()
        mod_all()
        if any(L % 2 == 0 for L in layers):
            for b in range(NSEQ):
                rope_tables(b)
        for li, L in enumerate(layers):
            layer_setup(L)
            for b in range(NSEQ):
                mod_load(L, b)
                p1_phase(L, b, li == 0)
                if L % 2 == 0:
                    p2a_phase(L, b)
                else:
                    p2b_phase(L, b)
                p3_phase(L, b, li == 0)
        S.emit()
        build.ninst = S.ninst
        build.nsem = len(S.sems)
    return nc


_NC_CACHE = {}


def kernel(x, c, positions, ada_w, ada_b, norm1_g, norm2_g, wqkv_a, q_norm_a, k_norm_a, sinks_a, wo_a,
           wqkv_b, wo_b, w_gate, w_up, w_down):
    f = lambda a: np.ascontiguousarray(np.asarray(a, dtype=np.float32))
    x = f(x)
    B, S_, D_ = x.shape
    nseq = B // N_CORES
    if "nc" not in _NC_CACHE:
        _NC_CACHE["nc"] = build(S_, nseq, (0, 1, 2, 3))
    nc = _NC_CACHE["nc"]
    shared = dict(ada_w=f(ada_w), ada_b=f(ada_b), norm1_g=f(norm1_g), norm2_g=f(norm2_g), wqkv_a=f(wqkv_a),
                  q_norm_a=f(q_norm_a), k_norm_a=f(k_norm_a), sinks_a=f(sinks_a), wo_a=f(wo_a), wqkv_b=f(wqkv_b),
                  wo_b=f(wo_b), w_gate=f(w_gate), w_up=f(w_up), w_down=f(w_down))
    c = f(c)
    positions = np.ascontiguousarray(np.asarray(positions, dtype=np.int32))
    in_maps = []
    for i in range(N_CORES):
        m = dict(shared)
        m["x"] = x[i * nseq:(i + 1) * nseq].reshape(nseq * S_, D_)
        m["c"] = c[i * nseq:(i + 1) * nseq]
        m["positions"] = positions[i * nseq:(i + 1) * nseq]
        in_maps.append(m)
    res = run_bass_kernel_spmd(nc, in_maps, core_ids=list(range(N_CORES)))
    outs = [np.asarray(r["out"], dtype=np.float32).reshape(nseq, S_, D_) for r in res.results]
    return np.concatenate(outs, axis=0)
```
